# Optimizing a Trainium2 kernel written in Bass

```python
import jax, jax.numpy as jnp
from jax import lax
import numpy as np

D_MODEL = 1024
BATCH = 2
SEQ = 8192
DEPTH = 1

HEAD_DIM = 64
N_Q_HEADS = D_MODEL // HEAD_DIM
N_KV_HEADS = N_Q_HEADS // 8
GROUP = N_Q_HEADS // N_KV_HEADS
WINDOW = 128
BLOCK = 128
ATTN_WIDTH = N_Q_HEADS * HEAD_DIM
KV_WIDTH = N_KV_HEADS * HEAD_DIM
CONV_WIDTH = D_MODEL
CONV_K = 3
N_BRANCH = 2
D_FF = -(-8 * D_MODEL // (3 * 256)) * 256
IN_WIDTH = ATTN_WIDTH + 2 * KV_WIDTH + 3 * CONV_WIDTH + N_BRANCH * D_MODEL
N_MOD = 6
EPS = 1e-6

kernel_name = "hybrid_swa_sink_shortconv_gated_merge_adaln_block"


def rms_norm(x, g):
    xf = x.astype(jnp.float32)
    y = xf * lax.rsqrt(jnp.mean(xf * xf, axis=-1, keepdims=True) + EPS)
    return (y * g.astype(jnp.float32)).astype(x.dtype)


def with_prev_block(t):
    prev = jnp.pad(t, ((0, 0), (1, 0), (0, 0), (0, 0), (0, 0)))[:, :-1]
    return jnp.concatenate([prev, t], axis=2)


def sliding_window_sink_attention(q, k, v, sinks):
    B, T = q.shape[0], q.shape[1]
    nb = T // BLOCK
    qb = q.reshape(B, nb, BLOCK, N_KV_HEADS, GROUP, HEAD_DIM)
    kx = with_prev_block(k.reshape(B, nb, BLOCK, N_KV_HEADS, HEAD_DIM))
    vx = with_prev_block(v.reshape(B, nb, BLOCK, N_KV_HEADS, HEAD_DIM))
    s = jnp.einsum('bnqhgd,bnkhd->bnhgqk', qb, kx).astype(jnp.float32) * (HEAD_DIM ** -0.5)
    qpos = jnp.arange(BLOCK)[:, None] + BLOCK
    kpos = jnp.arange(2 * BLOCK)[None, :]
    rel = qpos - kpos
    band = (rel >= 0) & (rel < WINDOW)
    blk = jnp.arange(nb)[:, None, None]
    valid = band[None] & ((blk > 0) | (kpos[None] >= BLOCK))
    s = jnp.where(valid[None, :, None, None], s, -jnp.inf)
    sink = sinks.astype(jnp.float32).reshape(1, 1, N_KV_HEADS, GROUP, 1, 1)
    m = jnp.maximum(jnp.max(s, axis=-1, keepdims=True), sink)
    p = jnp.exp(s - m)
    denom = jnp.sum(p, axis=-1, keepdims=True) + jnp.exp(sink - m)
    p = (p / denom).astype(v.dtype)
    o = jnp.einsum('bnhgqk,bnkhd->bnqhgd', p, vx)
    return o.reshape(B, T, ATTN_WIDTH)


def causal_depthwise_conv(u, w):
    C = u.shape[-1]
    return lax.conv_general_dilated(
        u, w[:, None, :].astype(u.dtype), window_strides=(1,),
        padding=[(CONV_K - 1, 0)], dimension_numbers=('NWC', 'WIO', 'NWC'),
        feature_group_count=C)


def setup_inputs(seed: int = 0) -> dict:
    key = jax.random.key(seed)
    ks = jax.random.split(key, 16)
    f32 = jnp.float32
    nrm = lambda k, shape, s: jax.random.normal(k, shape, f32) * s
    return {
        "x": nrm(ks[0], (BATCH, SEQ, D_MODEL), 1.0),
        "c": nrm(ks[1], (BATCH, D_MODEL), 1.0),
        "w_ada": nrm(ks[2], (DEPTH, D_MODEL, N_MOD * D_MODEL), D_MODEL ** -0.5),
        "b_ada": nrm(ks[3], (DEPTH, N_MOD * D_MODEL), 0.02),
        "g_mix": 1.0 + nrm(ks[4], (DEPTH, D_MODEL), 0.02),
        "w_in": nrm(ks[5], (DEPTH, D_MODEL, IN_WIDTH), D_MODEL ** -0.5),
        "b_in": nrm(ks[6], (DEPTH, IN_WIDTH), 0.02),
        "sinks": nrm(ks[7], (DEPTH, N_Q_HEADS), 0.5),
        "conv_w": nrm(ks[8], (DEPTH, CONV_K, CONV_WIDTH), CONV_K ** -0.5),
        "w_out": nrm(ks[9], (DEPTH, D_MODEL, D_MODEL), D_MODEL ** -0.5),
        "g_ffn": 1.0 + nrm(ks[10], (DEPTH, D_MODEL), 0.02),
        "w_ffn_in": nrm(ks[11], (DEPTH, D_MODEL, 2 * D_FF), D_MODEL ** -0.5),
        "w_ffn_out": nrm(ks[12], (DEPTH, D_FF, D_MODEL), D_FF ** -0.5),
        "g_final": 1.0 + nrm(ks[13], (D_MODEL,), 0.02),
    }


def reference(x, c, w_ada, b_ada, g_mix, w_in, b_in, sinks, conv_w, w_out,
              g_ffn, w_ffn_in, w_ffn_out, g_final):
    splits = [ATTN_WIDTH,
              ATTN_WIDTH + KV_WIDTH,
              ATTN_WIDTH + 2 * KV_WIDTH,
              ATTN_WIDTH + 2 * KV_WIDTH + CONV_WIDTH,
              ATTN_WIDTH + 2 * KV_WIDTH + 2 * CONV_WIDTH,
              ATTN_WIDTH + 2 * KV_WIDTH + 3 * CONV_WIDTH,
              ATTN_WIDTH + 2 * KV_WIDTH + 3 * CONV_WIDTH + D_MODEL]
    for l in range(DEPTH):
        mod = (jax.nn.silu(c) @ w_ada[l] + b_ada[l])[:, None, :]
        sh1, sc1, ga1, sh2, sc2, ga2 = jnp.split(mod, N_MOD, axis=-1)

        h = rms_norm(x, g_mix[l]) * (1 + sc1) + sh1
        z = h @ w_in[l] + b_in[l]
        q, k, v, conv_b, conv_c, conv_x, gate_a, gate_c = jnp.split(z, splits, axis=-1)
        attn = sliding_window_sink_attention(q, k, v, sinks[l])
        conv = conv_b * causal_depthwise_conv(conv_c * conv_x, conv_w[l])
        merged = jax.nn.sigmoid(gate_a) * attn + jax.nn.sigmoid(gate_c) * conv
        x = x + ga1 * (merged @ w_out[l])

        h = rms_norm(x, g_ffn[l]) * (1 + sc2) + sh2
        gu = h @ w_ffn_in[l]
        g_part, u_part = jnp.split(gu, 2, axis=-1)
        x = x + ga2 * ((jax.nn.silu(g_part) * u_part) @ w_ffn_out[l])
    return rms_norm(x, g_final)
```

```python
import contextlib
import numpy as np
import concourse.bass as bass
import concourse.mybir as mybir
from concourse.bass_utils import run_bass_kernel_spmd

F32 = mybir.dt.float32
BF16 = mybir.dt.bfloat16
AF = mybir.ActivationFunctionType
ALU = mybir.AluOpType

D = 1024
NT = 2048
NB = 16
NTH = NT + 128
DFF = 2816
NFC = 22
FGROUPS = [list(range(0, 8)), list(range(8, 15)), list(range(15, 22))]
NZ = 51
EPS = 1e-6
NCORES = 8


class StopEmit(Exception):
    pass


STAGE = 9
DO_ATTN = True
DEFER_TAPS = True
MASK_ENG = 'pool'
PV_LAG = 4
TAIL_DB = True
PV_MERGE = True
TAPS_ON_ACT = True
NPT = 5
NPROJ_BANKS = 4
LATE_AT_B = False
ACT_OEVAC = True
DO_PV = True
DO_NORM = True
DO_DEN = True
DO_HALO = True
NCHUNK_DBG = 8


def checkpoint(n):
    if STAGE < n:
        raise StopEmit()


class Ev:
    __slots__ = ("sem", "value")

    def __init__(self, sem, value):
        self.sem = sem
        self.value = value


class Prog:
    ENGS = ("pe", "act", "dve", "pool", "sp")

    def __init__(self):
        self.lists = {e: [] for e in self.ENGS}
        self.cnt = {e: 0 for e in self.ENGS}
        self.pending = {e: [] for e in self.ENGS}
        self.lastw = {}
        self.readers = {}
        self.dma_cnt = {}
        self.last_ev = {}

    def op(self, eng, fn, reads=(), writes=(), sig=True, dma=None, serialize=True):
        waits = []
        for r in reads:
            if r in self.lastw:
                waits.append((self.lastw[r], "raw"))
        for w in writes:
            if w in self.lastw:
                waits.append((self.lastw[w], "waw"))
            for ev in self.readers.get(w, {}).values():
                waits.append((ev, "war"))
        if dma is not None:
            sres = "sem:" + dma
            if serialize and sres in self.lastw:
                waits.append((self.lastw[sres], "waw"))
            self.dma_cnt[dma] = self.dma_cnt.get(dma, 0) + 16
            ev = Ev(dma, self.dma_cnt[dma])
            self.lastw[sres] = ev
        else:
            ev = Ev(eng, None)
            self.pending[eng].append(ev)
            if sig:
                self.cnt[eng] += 1
                for p in self.pending[eng]:
                    p.value = self.cnt[eng]
                self.pending[eng] = []
            self.last_ev[eng] = ev
        for r in reads:
            self.readers.setdefault(r, {})[ev.sem] = ev
        for w in writes:
            self.lastw[w] = ev
            self.readers[w] = {}
        self.lists[eng].append((fn, waits, sig, dma))
        return ev

    def fence(self, resources, engines=("pe", "act", "dve", "pool")):
        for r in resources:
            d = self.readers.setdefault(r, {})
            for e in engines:
                if e in self.last_ev:
                    d[e] = self.last_ev[e]

    def replay(self, eng, engine, sems):
        waited = {}
        for fn, waits, sig, dma in self.lists[eng]:
            need = {}
            for ev, kind in waits:
                if ev.value is None:
                    raise RuntimeError("unresolved event on " + ev.sem)
                if ev.sem == eng:
                    if eng == "pe" or kind == "war":
                        continue
                need[ev.sem] = max(need.get(ev.sem, 0), ev.value)
            for s, v in need.items():
                if waited.get(s, 0) >= v:
                    continue
                engine.wait_ge(sems[s], v)
                waited[s] = v
            ins = fn(engine)
            if dma is not None:
                ins.then_inc(sems[dma], 16)
            elif sig:
                ins.then_inc(sems[eng], 1)


def build_program():
    nc = bass.Bass("TRN2", target_bir_lowering=False)
    P = Prog()

    def din(name, shape):
        return nc.dram_tensor(name, list(shape), F32, kind="ExternalInput").ap()

    xh = din("xh", [NTH, D])
    c_col_d = din("c_col", [128, 8])
    w_ada = din("w_ada", [D, 6 * D])
    b_ada_col_d = din("b_ada_col", [128, 48])
    gmix_col_d = din("gmix_col", [128, 8])
    gffn_col_d = din("gffn_col", [128, 8])
    w_in_perm = din("w_in_perm", [NZ, 128, 1024])
    b_in_col_d = din("b_in_col", [128, NZ])
    bv_bc_d = din("bv_bc", [128, 128])
    sink_col_d = din("sink_col", [128, 8])
    convw_col_d = din("convw_col", [128, 24])
    halo_flag_d = din("halo_flag", [128, 1])
    mask_cp_d = din("mask_cp", [128, 256])
    mask_halo_d = din("mask_halo", [128, 128])
    ident_d = din("ident", [128, 128])
    w_out_d = din("w_out", [D, D])
    ga1_b_d = din("ga1_b_bc", [128, D])
    ga2_b_d = din("ga2_b_bc", [128, D])
    gf_bc_d = din("gf_bc", [128, D])
    wfi_perm = din("wfi_perm", [2 * NFC, 128, 1024])
    w_ffn_out = din("w_ffn_out", [DFF, D])
    out_d = nc.dram_tensor("out", [NT, D], F32, kind="ExternalOutput").ap()

    es = contextlib.ExitStack()

    def sb(name, shape, dt):
        return es.enter_context(nc.sbuf_tensor(name, list(shape), dt))

    with es:
        R_h = sb("R_h", [128, 8 * NTH], BF16)
        R_A = sb("R_A", [128, 32768], BF16)
        R_B = sb("R_B", [128, 24576 + 1024], BF16)
        R_W = sb("R_W", [128, 14336], BF16)
        xb = sb("xb", [128, 2, 1024], F32)
        xn = sb("xn", [128, 2, 1024], BF16)
        GA1 = sb("GA1", [128, D], F32)
        GA2 = sb("GA2", [128, D], F32)
        GF = sb("GF", [128, D], F32)
        ident = sb("ident_bf", [128, 128], BF16)
        mask_cp = sb("mask_cp_bf", [128, 256], BF16)
        mask_halo = sb("mask_halo_bf", [128, 128], BF16)
        ones_bf = sb("ones_bf", [128, 128], BF16)
        c_col = sb("c_col_sb", [128, 8], F32)
        sc_bf = sb("sc_bf", [128, 8], BF16)
        sc_f = sb("sc_f", [128, 8], F32)
        sc_rep = sb("sc_rep", [128, 8, 128], BF16)
        b_ada_col = sb("b_ada_col_sb", [128, 48], F32)
        gmix_col = sb("gmix_col_sb", [128, 8], F32)
        gffn_col = sb("gffn_col_sb", [128, 8], F32)
        b_in_col = sb("b_in_col_sb", [128, NZ], F32)
        hb_col = sb("hb_col", [128, NZ], F32)
        bv_bc = sb("bv_bc_sb", [128, 128], F32)
        sink_col = sb("sink_col_sb", [128, 8], F32)
        esink = sb("esink", [128, 8], F32)
        convw = sb("convw_sb", [128, 24], F32)
        halo_flag = sb("halo_flag_sb", [128, 1], F32)
        modcol = sb("modcol", [128, 48], F32)
        a1col = sb("a1col", [128, 8], F32)
        a2col = sb("a2col", [128, 8], F32)
        eps_col = sb("eps_col", [128, 1], F32)
        ss = sb("ss", [128, 64], F32)
        sq = sb("sq", [128, 64], F32)
        rstd = sb("rstd", [128, 64], F32)
        small = sb("small", [128, 16], F32)
        sgt = sb("sgt", [128, 2, 512], BF16)
        psum = es.enter_context(nc.psum_tensor("psum", [128, 4096], F32))

        h = R_h[:].rearrange("p (k t) -> p k t", k=8)
        RA32 = R_A[:].bitcast(F32)
        x1 = RA32.rearrange("p (n d) -> p n d", d=1024)
        def st8(ap):
            return ap.rearrange("p (k n) -> p k n", k=8)
        wada_st = [st8(R_A[:, 0:8192]), st8(R_A[:, 8192:16384]), st8(R_A[:, 21520:29712]),
                   st8(R_W[:, 6144:14336]), st8(R_B[:, 0:8192]), st8(R_B[:, 8192:16384])]
        cx_t = RA32[:, 0:2048]
        u_t = RA32[:, 2048:4104]
        cva_t = [RA32[:, 4104:6152], RA32[:, 10760:12808]]
        ta_t = [RA32[:, 6152:8200], RA32[:, 12808:14856]]
        qT = [R_A[:, 16400:18448], R_A[:, 29712:31760]]
        rden_t = [RA32[:, 9224:9480], RA32[:, 9480:9736]]
        nrm_t = [RA32[:, 9736:9992], RA32[:, 9992:10248]]
        mergedT = R_B[:, 0:16384].rearrange("p (k t) -> p k t", k=8)
        aT = R_B[:, 0:16384].rearrange("p (k t) -> p k t", k=8)
        kT = R_B[:, 16384:16384 + 2 * NTH].rearrange("p (g t) -> p g t", g=2)
        o1 = 16384 + 2 * NTH
        v_sb = R_B[:, o1:o1 + 17 * 128].rearrange("p (b n) -> p b n", n=128)
        o2 = o1 + 17 * 128
        pT = [R_B[:, o2 + i * 512:o2 + (i + 1) * 512].rearrange("p (h q) -> p h q", h=2) for i in range(NPT)]
        wfo = [R_B[:, 16384 + i * 1024:16384 + (i + 1) * 1024] for i in range(8)]
        ws = [R_W[:, i * 1024:(i + 1) * 1024].rearrange("p (k n) -> p k n", k=8) for i in range(6)]
        wout = R_W[:, 6144:14336].rearrange("p (k n) -> p k n", k=8)

        xq = R_B[:, 16384:20480].bitcast(F32)
        xslot = [xb[:, 0, :], xb[:, 1, :], xq[:, 0:1024], xq[:, 1024:2048]]
        mask3 = mask_cp[:].rearrange("p (o q) -> p o q", o=1)
        mask_halo3 = mask_halo[:].rearrange("p (o q) -> p o q", o=1)

        def bank(b, n=1):
            return psum[:, b * 512:(b + n) * 512]

        def bank_bf(b):
            return psum[:, b * 512:(b + 1) * 512].bitcast(BF16).rearrange("p (k t) -> p k t", k=8)

        def hres(k, tok0, ntok):
            return [f"h{k}_{t}" for t in range(tok0 // 128, (tok0 + ntok + 127) // 128)]

        final_waits = []
        try:
            const_res = []

            def cload(eng, dst, src, res):
                P.op(eng, lambda e, dst=dst, src=src: e.dma_start(out=dst, in_=src), writes=[res], dma="const_" + eng, serialize=False)
                const_res.append((res, "const_" + eng))

            cload("sp", c_col[:], c_col_d, "c_col")
            cload("sp", b_ada_col[:], b_ada_col_d, "b_ada_col")
            cload("sp", gmix_col[:], gmix_col_d, "gmix_col")
            cload("sp", gffn_col[:], gffn_col_d, "gffn_col")
            cload("sp", b_in_col[:], b_in_col_d, "b_in_col")
            cload("sp", bv_bc[:], bv_bc_d, "bv_bc")
            cload("sp", sink_col[:], sink_col_d, "sink_col")
            cload("sp", convw[:], convw_col_d, "convw")
            cload("sp", halo_flag[:], halo_flag_d, "halo_flag")
            cload("sp", GA1[:], ga1_b_d, "GA1")
            cload("sp", GA2[:], ga2_b_d, "GA2")
            cload("sp", GF[:], gf_bc_d, "GF")
            cload("pool", ident[:], ident_d, "ident")
            cload("pool", mask_cp[:], mask_cp_d, "mask_cp")
            cload("pool", mask_halo[:], mask_halo_d, "mask_halo")
            for res, s in const_res:
                P.lastw[res] = Ev(s, P.dma_cnt[s])

            wada_v = w_ada.rearrange("(k p) n -> p k n", p=128)
            def load_wada(v, after=()):
                for hf in range(2):
                    P.op("pool", lambda e, hf=hf: e.dma_start(out=wada_st[v][:, 4 * hf:4 * hf + 4, :],
                                                              in_=wada_v[:, 4 * hf:4 * hf + 4, v * 1024:(v + 1) * 1024]),
                         reads=list(after), writes=[f"wada_st{v}_{hf}"], dma=f"wst{v}_{hf}")

            NZI = NZ + 2 * NFC
            loaded = set()

            def load_z(zi):
                if zi >= NZI or zi in loaded:
                    return
                loaded.add(zi)
                i = zi % 6
                src_ap = w_in_perm[zi] if zi < NZ else wfi_perm[zi - NZ]
                P.op("pool", lambda e: e.dma_start(out=ws[i], in_=src_ap.rearrange("p (k n) -> p k n", k=8)),
                     writes=[f"ws{i}"], dma=f"ws{i}")

            load_wada(0)
            load_wada(1)
            for zi in range(6):
                load_z(zi)

            P.op("dve", lambda e: e.memset(eps_col[:], EPS), writes=["eps_col"])
            P.op("dve", lambda e: e.memset(ones_bf[:], 1.0), writes=["ones_bf"])
            P.op("act", lambda e: e.activation(out=sc_f[:], in_=c_col[:], func=AF.Silu), reads=["c_col"], writes=["sc_f"])
            P.op("dve", lambda e: e.tensor_copy(out=sc_bf[:], in_=sc_f[:]), reads=["sc_f"], writes=["sc_bf"])
            for k in range(8):
                P.op("dve", lambda e, k=k: e.tensor_scalar(out=sc_rep[:, k, :], in0=ones_bf[:], scalar1=sc_f[:, k:k + 1], scalar2=None, op0=ALU.mult),
                     reads=["sc_f", "ones_bf"], writes=["sc_rep"])
            P.op("dve", lambda e: e.tensor_scalar(out=hb_col[:], in0=b_in_col[:], scalar1=0.5, scalar2=None, op0=ALU.mult),
                 reads=["b_in_col"], writes=["hb_col"])
            P.op("act", lambda e: e.activation(out=esink[:], in_=sink_col[:], func=AF.Exp), reads=["sink_col"], writes=["esink"])

            def mod_columns(v):
                for j in range(8):
                    for k in range(8):
                        P.op("pe", lambda e, j=j, k=k: e.matmul(psum[:, 3 * 512 + v * 8 + j:3 * 512 + v * 8 + j + 1],
                                                                lhsT=wada_st[v][:, k, j * 128:(j + 1) * 128],
                                                                rhs=sc_bf[:, k:k + 1], start=(k == 0), stop=(k == 7)),
                             reads=[f"wada_st{v}_{k // 4}", "sc_bf"], writes=["ps3"], sig=(k == 7 and j == 7))
                P.op("dve", lambda e: e.tensor_tensor(out=modcol[:, v * 8:v * 8 + 8], in0=psum[:, 3 * 512 + v * 8:3 * 512 + v * 8 + 8],
                                                      in1=b_ada_col[:, v * 8:v * 8 + 8], op=ALU.add),
                     reads=["b_ada_col", "ps3"], writes=[f"modcol{v}"])

            def mod_bcast(v, GA, gres, scale):
                for hf in range(2):
                    for k in range(8):
                        P.op("pe", lambda e, hf=hf, k=k: e.matmul(bank(hf), lhsT=sc_rep[:, k, :], rhs=wada_st[v][:, k, hf * 512:(hf + 1) * 512],
                                                                  start=(k == 0), stop=(k == 7)),
                             reads=[f"wada_st{v}_{k // 4}", "sc_rep"], writes=[f"ps{hf}"], sig=(k == 7))
                    P.op("dve", lambda e, hf=hf: e.tensor_tensor(out=GA[:, hf * 512:(hf + 1) * 512], in0=bank(hf),
                                                                 in1=GA[:, hf * 512:(hf + 1) * 512], op=ALU.add),
                         reads=[gres, f"ps{hf}"], writes=[gres])
                if scale != 1.0:
                    P.op("dve", lambda e: e.tensor_scalar(out=GA[:], in0=GA[:], scalar1=scale, scalar2=None, op0=ALU.mult),
                         reads=[gres], writes=[gres])

            mod_columns(0)
            mod_columns(1)
            P.op("dve", lambda e: e.scalar_tensor_tensor(out=a1col[:], in0=modcol[:, 8:16], scalar=1.0, in1=gmix_col[:],
                                                         op0=ALU.add, op1=ALU.mult),
                 reads=["modcol1", "gmix_col"], writes=["a1col"])

            junk_bf = sgt[:].rearrange("p a b -> p (a b)")

            def nt_stage1(src_ap, src_res, ssi):
                P.op("act", lambda e: e.activation(out=junk_bf, in_=src_ap, func=AF.Square, accum_out=ss[:, ssi:ssi + 1]),
                     reads=[src_res], writes=[f"ss{ssi}"])
                P.op("act", lambda e: e.activation(out=sq[:, ssi:ssi + 1], in_=ss[:, ssi:ssi + 1], func=AF.Sqrt,
                                                   bias=eps_col[:], scale=1.0 / D),
                     reads=[f"ss{ssi}", "eps_col"], writes=[f"sq{ssi}"])

            def nt_recip(ssi):
                P.op("dve", lambda e: e.reciprocal(out=rstd[:, ssi:ssi + 1], in_=sq[:, ssi:ssi + 1]),
                     reads=[f"sq{ssi}"], writes=[f"rstd{ssi}"])

            def nt_copy(src_ap, src_res, ssi):
                xi = ssi % 2
                P.op("act", lambda e: e.activation(out=xn[:, xi, :], in_=src_ap, func=AF.Copy, scale=rstd[:, ssi:ssi + 1]),
                     reads=[src_res, f"rstd{ssi}"], writes=[f"xn{xi}"])

            def nt_transposes(ssi, pbank):
                xi = ssi % 2
                pb = bank_bf(pbank)
                for k in range(8):
                    P.op("pe", lambda e, k=k: e.transpose(pb[:, k, :], xn[:, xi, k * 128:(k + 1) * 128], ident[:]),
                         reads=[f"xn{xi}", "ident"], writes=[f"ps{pbank}"], sig=(k == 7))

            def nt_stage2(src_ap, src_res, ssi, pbank):
                nt_recip(ssi)
                nt_copy(src_ap, src_res, ssi)
                nt_transposes(ssi, pbank)

            def nt_stage3(t_h, acol, shcol, acol_res, shcol_res, pbank):
                pb = bank_bf(pbank)
                for k in range(8):
                    P.op("dve", lambda e, k=k: e.tensor_scalar(out=h[:, k, t_h * 128:(t_h + 1) * 128], in0=pb[:, k, :],
                                                               scalar1=acol[:, k:k + 1], scalar2=shcol[:, k:k + 1],
                                                               op0=ALU.mult, op1=ALU.add),
                         reads=[acol_res, shcol_res, f"ps{pbank}"], writes=[f"h{k}_{t_h}"])

            def emit_late_mods():
                mod_columns(3)
                mod_columns(4)
                P.op("dve", lambda e: e.scalar_tensor_tensor(out=a2col[:], in0=modcol[:, 32:40], scalar=1.0, in1=gffn_col[:],
                                                             op0=ALU.add, op1=ALU.mult),
                     reads=["modcol4", "gffn_col"], writes=["a2col"])
                mod_bcast(2, GA1, "GA1", 0.5)
                mod_bcast(5, GA2, "GA2", 1.0)
                tiles = [0, 512, 1024, 1536]
                P.fence([f"cva1_{t}" for t in tiles] + [f"ta1_{t}" for t in tiles] + [f"qT1_{t}" for t in tiles]
                        + [f"mg{c}_{g}" for c in range(8) for g in range(8)] + ["wout0", "wout1"], engines=("pe",))

            checkpoint(1)
            for i in range(17 + 2):
                if i < 17:
                    t = i
                    xi = t % 4
                    P.op("sp", lambda e, t=t, xi=xi: e.dma_start(out=xslot[xi], in_=xh[t * 128:(t + 1) * 128, :]),
                         writes=[f"xb{xi}"], dma=f"xb{xi}")
                    nt_stage1(xslot[xi], f"xb{xi}", t)
                    nt_stage2(xslot[xi], f"xb{xi}", t, 4 + t % 4)
                if 0 <= i - 1 < 17:
                    t = i - 1
                    nt_stage3(t, a1col[:], modcol[:, 0:8], "a1col", "modcol0", 4 + t % 4)
                if i == 9 and LATE_AT_B:
                    emit_late_mods()
            for v in (3, 4, 5, 2):
                load_wada(v, after=[f"xb{i}" for i in range(4)])
            tiles_ = [0, 512, 1024, 1536]
            P.fence([f"cx_{t}" for t in tiles_] + [f"u_{t}" for t in tiles_] + ["u_h"] + [f"cva0_{t}" for t in tiles_]
                    + [f"ta0_{t}" for t in tiles_] + [f"qT0_{t}" for t in tiles_] + [f"rden{i}" for i in range(2)]
                    + [f"nrm{i}" for i in range(2)], engines=("pe",))

            checkpoint(2)
            wout_v = w_out_d.rearrange("(k p) n -> p k n", p=128)

            proj_bank = [0]

            def proj_group(slot, tok0, ntok, evac):
                b = proj_bank[0] % NPROJ_BANKS
                proj_bank[0] += 1
                for k in range(8):
                    P.op("pe", lambda e, k=k: e.matmul(bank(b)[:, 0:ntok], lhsT=ws[slot][:, k, :], rhs=h[:, k, tok0:tok0 + ntok],
                                                       start=(k == 0), stop=(k == 7)),
                         reads=[f"ws{slot}"] + hres(k, tok0, ntok), writes=[f"ps{b}"], sig=(k == 7))
                evac(b)

            for g in range(2):
                slot = g
                for tok0, ntok in [(0, 128), (128, 512), (640, 512), (1152, 512), (1664, 512)]:
                    def ev_k(b, g=g, tok0=tok0, ntok=ntok):
                        P.op("act", lambda e: e.activation(out=kT[:, g, tok0:tok0 + ntok], in_=bank(b)[:, 0:ntok], func=AF.Identity,
                                                           bias=b_in_col[:, g:g + 1]),
                             reads=["b_in_col", f"ps{b}"], writes=[f"kT{g}_{tok0}"])
                    proj_group(slot, tok0, ntok, ev_k)
                load_z(6 + g)

            def kres(g, tb):
                tok = tb * 128
                for tok0, ntok in [(0, 128), (128, 512), (640, 512), (1152, 512), (1664, 512)]:
                    if tok0 <= tok < tok0 + ntok:
                        return f"kT{g}_{tok0}"

            vslot = 2
            for tb in range(17):
                for k in range(8):
                    P.op("pe", lambda e, tb=tb, k=k: e.matmul(bank(3)[:, 0:128], lhsT=h[:, k, tb * 128:(tb + 1) * 128], rhs=ws[vslot][:, k, :],
                                                              start=(k == 0), stop=(k == 7)),
                         reads=[f"ws{vslot}", f"h{k}_{tb}"], writes=["ps3"], sig=(k == 7))
                P.op("dve", lambda e, tb=tb: e.tensor_tensor(out=v_sb[:, tb, :], in0=bank(3)[:, 0:128], in1=bv_bc[:], op=ALU.add),
                     reads=["bv_bc", "ps3"], writes=[f"v{tb}"])

            load_z(8)
            deferred = []
            tap_q = [[], []]
            checkpoint(3)
            TT = [(0, 512), (512, 512), (1024, 512), (1536, 512)]

            def make_chunk_groups(c):
                groups = []
                par = c % 2
                zb = 3 + 6 * c
                def need_slot(name, zi):
                    load_z(zi)
                    return zi % 6

                def prefetch(t):
                    deferred.append(lambda: load_z(zb + 6 + t))

                for (t0, nt) in TT:
                    def g_cx(t0=t0, nt=nt):
                        s = need_slot("cx", zb + 0)

                        def evac(b):
                            P.op("act", lambda e: e.activation(out=cx_t[:, t0:t0 + nt], in_=bank(b), func=AF.Identity,
                                                               bias=b_in_col[:, zb:zb + 1]),
                                 reads=["b_in_col", f"ps{b}"], writes=[f"cx_{t0}"])
                        proj_group(s, 128 + t0, nt, evac)
                    groups.append(g_cx)

                def g_halo():
                    s_cx = need_slot("cx", zb + 0)
                    s_cc = need_slot("cc", zb + 1)
                    for idx, s in enumerate((s_cx, s_cc)):
                        for k in range(8):
                            P.op("pe", lambda e, k=k, s=s, idx=idx: e.matmul(bank(3)[:, 400 + 2 * idx:402 + 2 * idx], lhsT=ws[s][:, k, :],
                                                                              rhs=h[:, k, 126:128], start=(k == 0), stop=(k == 7)),
                                 reads=[f"ws{s}", f"h{k}_0"], writes=["ps3"], sig=(k == 7))
                    P.op("dve", lambda e: e.tensor_scalar(out=small[:, 0:2], in0=bank(3)[:, 400:402], scalar1=b_in_col[:, zb:zb + 1],
                                                          scalar2=halo_flag[:, 0:1], op0=ALU.add, op1=ALU.mult),
                         reads=["b_in_col", "halo_flag", "ps3"], writes=["small01"])
                    P.op("dve", lambda e: e.scalar_tensor_tensor(out=u_t[:, 0:2], in0=bank(3)[:, 402:404], scalar=b_in_col[:, zb + 1:zb + 2],
                                                                 in1=small[:, 0:2], op0=ALU.add, op1=ALU.mult),
                         reads=["b_in_col", "small01", "ps3"], writes=["u_h"])
                    prefetch(0)
                if DO_HALO:
                    groups.append(g_halo)

                for ti, (t0, nt) in enumerate(TT):
                    def g_cc(t0=t0, nt=nt, ti=ti):
                        s = need_slot("cc", zb + 1)

                        def evac(b):
                            P.op("dve", lambda e: e.scalar_tensor_tensor(out=u_t[:, 2 + t0:2 + t0 + nt], in0=bank(b), scalar=b_in_col[:, zb + 1:zb + 2],
                                                                         in1=cx_t[:, t0:t0 + nt], op0=ALU.add, op1=ALU.mult),
                                 reads=["b_in_col", f"cx_{t0}", f"ps{b}"], writes=[f"u_{t0}"])
                            prev = ["u_h"] if ti == 0 else [f"u_{TT[ti - 1][0]}"]
                            cva = cva_t[par]
                            if DEFER_TAPS:
                                tap_q[0].append(lambda: taps(cva, prev))
                            else:
                                taps(cva, prev)

                        def taps(cva, prev):
                            P.op("act", lambda e: e.activation(out=cva[:, t0:t0 + nt], in_=u_t[:, t0:t0 + nt], func=AF.Copy,
                                                               scale=convw[:, c:c + 1]),
                                 reads=[f"u_{t0}", "convw"] + prev, writes=[f"cva{par}_{t0}"])
                            if TAPS_ON_ACT:
                                j = ti % 2
                                for tap in (1, 2):
                                    tmp = xb[:, j, (tap - 1) * 512:(tap - 1) * 512 + nt]
                                    P.op("act", lambda e, tap=tap, tmp=tmp: e.activation(out=tmp, in_=u_t[:, tap + t0:tap + t0 + nt], func=AF.Copy,
                                                                                         scale=convw[:, 8 * tap + c:8 * tap + c + 1]),
                                         reads=[f"u_{t0}", "convw"] + (prev if tap == 1 else []), writes=[f"tp{j}_{tap}"])
                                    P.op("dve", lambda e, tmp=tmp: e.tensor_tensor(out=cva[:, t0:t0 + nt], in0=cva[:, t0:t0 + nt], in1=tmp, op=ALU.add),
                                         reads=[f"tp{j}_{tap}", f"cva{par}_{t0}"], writes=[f"cva{par}_{t0}"])
                            else:
                                P.op("dve", lambda e: e.scalar_tensor_tensor(out=cva[:, t0:t0 + nt], in0=u_t[:, 1 + t0:1 + t0 + nt],
                                                                             scalar=convw[:, 8 + c:9 + c], in1=cva[:, t0:t0 + nt],
                                                                             op0=ALU.mult, op1=ALU.add),
                                     reads=[f"u_{t0}", "convw", f"cva{par}_{t0}"] + prev, writes=[f"cva{par}_{t0}"])
                                P.op("dve", lambda e: e.scalar_tensor_tensor(out=cva[:, t0:t0 + nt], in0=u_t[:, 2 + t0:2 + t0 + nt],
                                                                             scalar=convw[:, 16 + c:17 + c], in1=cva[:, t0:t0 + nt],
                                                                             op0=ALU.mult, op1=ALU.add),
                                     reads=[f"u_{t0}", "convw", f"cva{par}_{t0}"], writes=[f"cva{par}_{t0}"])
                        proj_group(s, 128 + t0, nt, evac)
                        if ti == 3:
                            prefetch(1)
                    groups.append(g_cc)

                for (t0, nt) in TT:
                    def g_cb(t0=t0, nt=nt):
                        s = need_slot("cb", zb + 2)

                        def evac(b):
                            cva = cva_t[par]
                            P.op("dve", lambda e: e.scalar_tensor_tensor(out=cva[:, t0:t0 + nt], in0=bank(b), scalar=b_in_col[:, zb + 2:zb + 3],
                                                                         in1=cva[:, t0:t0 + nt], op0=ALU.add, op1=ALU.mult),
                                 reads=["b_in_col", f"cva{par}_{t0}", f"ps{b}"], writes=[f"cva{par}_{t0}"])
                        proj_group(s, 128 + t0, nt, evac)
                        if t0 == 1536:
                            prefetch(2)
                    groups.append(g_cb)

                for (t0, nt) in TT:
                    def g_gc(t0=t0, nt=nt):
                        s = need_slot("gc", zb + 3)

                        def evac(b):
                            cva = cva_t[par]
                            P.op("act", lambda e: e.activation(out=cx_t[:, t0:t0 + nt], in_=bank(b), func=AF.Tanh, scale=0.5,
                                                               bias=hb_col[:, zb + 3:zb + 4]),
                                 reads=["hb_col", f"ps{b}"], writes=[f"cx_{t0}"])
                            P.op("dve", lambda e: e.scalar_tensor_tensor(out=cva[:, t0:t0 + nt], in0=cx_t[:, t0:t0 + nt], scalar=1.0,
                                                                         in1=cva[:, t0:t0 + nt], op0=ALU.add, op1=ALU.mult),
                                 reads=[f"cx_{t0}", f"cva{par}_{t0}"], writes=[f"cva{par}_{t0}"])
                        proj_group(s, 128 + t0, nt, evac)
                        if t0 == 1536:
                            prefetch(3)
                    groups.append(g_gc)

                for (t0, nt) in TT:
                    def g_ga(t0=t0, nt=nt):
                        s = need_slot("ga", zb + 4)

                        def evac(b):
                            P.op("act", lambda e: e.activation(out=ta_t[par][:, t0:t0 + nt], in_=bank(b), func=AF.Tanh, scale=0.5,
                                                               bias=hb_col[:, zb + 4:zb + 5]),
                                 reads=["hb_col", f"ps{b}"], writes=[f"ta{par}_{t0}"])
                        proj_group(s, 128 + t0, nt, evac)
                        if t0 == 1536:
                            prefetch(4)
                    groups.append(g_ga)

                for (t0, nt) in TT:
                    def g_q(t0=t0, nt=nt):
                        s = need_slot("q", zb + 5)

                        def evac(b):
                            P.op("act", lambda e: e.activation(out=qT[par][:, t0:t0 + nt], in_=bank(b), func=AF.Identity,
                                                               bias=b_in_col[:, zb + 5:zb + 6]),
                                 reads=["b_in_col", f"ps{b}"], writes=[f"qT{par}_{t0}"])
                        proj_group(s, 128 + t0, nt, evac)
                        if t0 == 1536:
                            prefetch(5)
                    groups.append(g_q)
                return groups

            pT_ctr = [0]

            def make_attn_steps(c):
                par = c % 2
                g = c // 4
                state = {}

                def scores(kb):
                    slot = pT_ctr[0] % NPT
                    pT_ctr[0] += 1
                    state[kb] = slot
                    parts = ([0] if kb >= 0 else []) + ([1] if kb + 1 <= 15 else [])
                    c0, c1 = parts[0] * 128, parts[-1] * 128 + 128
                    qb0 = kb if kb >= 0 else 0
                    qtok0 = qb0 * 128
                    nq = c1 - c0
                    qres = [f"qT{par}_{((qtok0 + i * 128) // 512) * 512}" for i in range(nq // 128)]
                    sb0 = 2 if (c == 7 and TAIL_DB and kb % 2 == 0) else 4
                    for hd in range(2):
                        r0 = hd * 64
                        P.op("pe", lambda e, hd=hd, r0=r0: e.matmul(bank(sb0 + hd)[:, c0:c1], lhsT=kT[r0:r0 + 64, g, (kb + 1) * 128:(kb + 2) * 128],
                                                                    rhs=qT[par][r0:r0 + 64, qtok0:qtok0 + nq], start=True, stop=True,
                                                                    tile_position=(r0, 0)),
                             reads=[kres(g, kb + 1)] + qres, writes=[f"ps{sb0 + hd}"], sig=True)
                    for hd in range(2):
                        P.op("act", lambda e, hd=hd: e.activation(out=pT[slot][:, hd, c0:c1], in_=bank(sb0 + hd)[:, c0:c1], func=AF.Exp, scale=0.125),
                             reads=[f"ps{sb0 + hd}"], writes=[f"pT{slot}_{hd}"])
                    msk = mask3[:, :, c0:c1] if kb >= 0 else mask_halo3
                    P.op(MASK_ENG, lambda e: e.tensor_tensor(out=pT[slot][:, :, c0:c1], in0=pT[slot][:, :, c0:c1],
                                                             in1=msk.to_broadcast([128, 2, c1 - c0]), op=ALU.mult),
                         reads=["mask_cp", "mask_halo"], writes=[f"pT{slot}_0", f"pT{slot}_1"])

                def pv(kb):
                    slot = state[kb]
                    if PV_MERGE and kb >= 0 and kb % 2 == 0:
                        g2 = kb // 2
                        b = 6 + g2 % 2
                        for (lhs_fn, col0, res) in ((lambda: v_sb[:, kb + 1, g * 64:(g + 1) * 64], 0, f"v{kb + 1}"),
                                                    (lambda: ones_bf[:, 0:64], 256, "ones_bf")):
                            for hd in range(2):
                                r0 = hd * 64
                                P.op("pe", lambda e, hd=hd, r0=r0, lhs_fn=lhs_fn, col0=col0: e.matmul(
                                         bank(b)[r0:r0 + 64, col0:col0 + 256], lhsT=lhs_fn(), rhs=pT[slot][:, hd, 0:256],
                                         start=False, stop=False, skip_group_check=True, tile_position=(0, r0)),
                                     reads=[res, f"pT{slot}_{hd}"], writes=[f"ps{b}"], sig=(col0 == 256 and hd == 1))
                        return
                    parts = ([(0, kb)] if kb >= 0 else []) + ([(1, kb + 1)] if kb + 1 <= 15 else [])
                    for (pi, n) in parts:
                        pv_part(kb, slot, pi, n)

                def pv_part(kb, slot, pi, n):
                    if True:
                        g2 = n // 2
                        b = 6 + g2 % 2
                        ocol = (n % 2) * 128
                        dcol = 256 + ocol
                        first_in_bank = (pi == 1 and n % 2 == 0)
                        for hd in range(2):
                            r0 = hd * 64
                            P.op("pe", lambda e, hd=hd, r0=r0: e.matmul(bank(b)[r0:r0 + 64, ocol:ocol + 128], lhsT=v_sb[:, kb + 1, g * 64:(g + 1) * 64],
                                                                        rhs=pT[slot][:, hd, pi * 128:(pi + 1) * 128], start=first_in_bank, stop=(pi == 0),
                                                                        skip_group_check=True, tile_position=(0, r0)),
                                 reads=[f"v{kb + 1}", f"pT{slot}_{hd}"], writes=[f"ps{b}"], sig=False)
                        for hd in range(2):
                            r0 = hd * 64
                            P.op("pe", lambda e, hd=hd, r0=r0: e.matmul(bank(b)[r0:r0 + 64, dcol:dcol + 128], lhsT=ones_bf[:, 0:64],
                                                                        rhs=pT[slot][:, hd, pi * 128:(pi + 1) * 128], start=False, stop=(pi == 0),
                                                                        skip_group_check=True, tile_position=(0, r0)),
                                 reads=["ones_bf", f"pT{slot}_{hd}"], writes=[f"ps{b}"], sig=(hd == 1))
                        if pi == 0 and n % 2 == 1 and DO_NORM:
                            tok0 = g2 * 256
                            i2 = g2 % 2
                            tt0 = (tok0 // 512) * 512
                            P.op("act", lambda e: e.activation(out=rden_t[i2], in_=bank(b)[:, 256:512], func=AF.Identity,
                                                               bias=esink[:, c:c + 1]),
                                 reads=["esink", f"ps{b}"], writes=[f"rden{i2}"])
                            if ACT_OEVAC:
                                P.op("act", lambda e: e.activation(out=nrm_t[i2], in_=bank(b)[:, 0:256], func=AF.Copy),
                                     reads=[f"ps{b}"], writes=[f"nrm{i2}"])
                            P.op("dve", lambda e: e.reciprocal(out=rden_t[i2], in_=rden_t[i2]), reads=[f"rden{i2}"], writes=[f"rden{i2}"])
                            if ACT_OEVAC:
                                P.op("pool" if c == 7 else "dve",
                                     lambda e: e.tensor_tensor(out=nrm_t[i2], in0=nrm_t[i2], in1=rden_t[i2], op=ALU.mult),
                                     reads=[f"rden{i2}", f"nrm{i2}"], writes=[f"nrm{i2}"])
                            else:
                                P.op("dve", lambda e: e.tensor_tensor(out=nrm_t[i2], in0=bank(b)[:, 0:256], in1=rden_t[i2], op=ALU.mult),
                                     reads=[f"rden{i2}", f"ps{b}"], writes=[f"nrm{i2}"])
                            P.op("dve", lambda e: e.scalar_tensor_tensor(out=nrm_t[i2], in0=ta_t[par][:, tok0:tok0 + 256], scalar=1.0,
                                                                         in1=nrm_t[i2], op0=ALU.add, op1=ALU.mult),
                                 reads=[f"ta{par}_{tt0}", f"nrm{i2}"], writes=[f"nrm{i2}"])
                            P.op("pool" if c == 7 else "dve",
                                 lambda e: e.tensor_tensor(out=mergedT[:, c, tok0:tok0 + 256], in0=nrm_t[i2],
                                                           in1=cva_t[par][:, tok0:tok0 + 256], op=ALU.add),
                                 reads=[f"nrm{i2}", f"cva{par}_{tt0}"], writes=[f"mg{c}_{g2}"])

                steps = []
                for i in range(-1, 16 + PV_LAG):
                    def st(i=i):
                        if i <= 15:
                            scores(i)
                        if -1 <= i - PV_LAG <= 15 and DO_PV:
                            pv(i - PV_LAG)
                    steps.append(st)
                return steps

            for c in range(9):
                if c > NCHUNK_DBG:
                    break
                groups = make_chunk_groups(c) if c < min(8, NCHUNK_DBG) else []
                steps = make_attn_steps(c - 1) if (c >= 1 and DO_ATTN) else []
                n = max(len(groups), len(steps))
                for i in range(n):
                    old = list(deferred) + tap_q[1]
                    del deferred[:]
                    tap_q[1] = tap_q[0]
                    tap_q[0] = []
                    if i < len(groups):
                        groups[i]()
                    for f in old:
                        f()
                    if i < len(steps):
                        steps[i]()
                    if c == 2 and i % 3 == 0 and i // 3 < 8:
                        kk = i // 3
                        P.op("pool", lambda e, kk=kk: e.tensor_tensor(out=wout[:, kk, :], in0=wout[:, kk, :], in1=GA1[:], op=ALU.mult),
                             reads=["GA1"], writes=[f"wout{kk // 4}"])
                for f in deferred + tap_q[1] + tap_q[0]:
                    f()
                del deferred[:]
                tap_q[0], tap_q[1] = [], []
                if c == 7 and STAGE >= 4:
                    P.fence([f"x1_{n}" for n in range(9)])
                    for n in range(9):
                        P.op("sp", lambda e, n=n: e.dma_start(out=x1[:, n, :], in_=xh[128 + n * 128:128 + (n + 1) * 128, :]),
                             writes=[f"x1_{n}"], dma=f"xr{n % 4}")
                if c == 0:
                    if not LATE_AT_B:
                        emit_late_mods()
                    for hf in range(2):
                        P.op("pool", lambda e, hf=hf: e.dma_start(out=wout[:, 4 * hf:4 * hf + 4, :], in_=wout_v[:, 4 * hf:4 * hf + 4, :]),
                             writes=[f"wout{hf}"], dma=f"wout{hf}")

            sh2col = modcol[:, 24:32]
            sh2res = "modcol3"

            checkpoint(4)
            P.fence([f"x1_{n}" for n in range(9, NB)])
            for n in range(9, NB):
                P.op("sp", lambda e, n=n: e.dma_start(out=x1[:, n, :], in_=xh[128 + n * 128:128 + (n + 1) * 128, :]),
                     writes=[f"x1_{n}"], dma=f"xr{n % 4}")
            def d_mm(n):
                b0 = 2 * (n % 2)
                for k in range(8):
                    for hf in range(2):
                        P.op("pe", lambda e, k=k, hf=hf: e.matmul(bank(b0 + hf), lhsT=mergedT[:, k, n * 128:(n + 1) * 128],
                                                                   rhs=wout[:, k, hf * 512:(hf + 1) * 512], start=(k == 0), stop=(k == 7)),
                             reads=[f"mg{k}_{n // 2}", f"wout{k // 4}"], writes=[f"ps{b0 + hf}"], sig=(k == 7 and hf == 1))

            def d_resid(n):
                b0 = 2 * (n % 2)
                P.op("dve", lambda e: e.tensor_tensor(out=x1[:, n, :], in0=bank(b0, 2), in1=x1[:, n, :], op=ALU.add),
                     reads=[f"ps{b0}", f"ps{b0 + 1}"], writes=[f"x1_{n}"])

            for i in range(NB + 6):
                def ok(j):
                    return 0 <= j < NB
                if ok(i):
                    d_mm(i)
                if ok(i - 1):
                    d_resid(i - 1)
                if ok(i - 2):
                    n = i - 2
                    nt_stage1(x1[:, n, :], f"x1_{n}", 20 + n)
                if ok(i - 3):
                    nt_recip(20 + i - 3)
                if ok(i - 4):
                    n = i - 4
                    nt_copy(x1[:, n, :], f"x1_{n}", 20 + n)
                if ok(i - 5):
                    n = i - 5
                    nt_transposes(20 + n, 4 + n % 4)
                if ok(i - 6):
                    n = i - 6
                    nt_stage3(n, a2col[:], sh2col, "a2col", sh2res, 4 + n % 4)

            checkpoint(5)
            P.fence([f"aT{jl}_{tt}" for jl in range(8) for tt in range(4)] + [f"wfo{i}" for i in range(8)])
            pair_ctr = [0]
            for G, js in enumerate(FGROUPS):
                for jl, j in enumerate(js):
                    load_z(NZ + 2 * j)
                    load_z(NZ + 2 * j + 1)
                    sg_ = (NZ + 2 * j) % 6
                    su_ = (NZ + 2 * j + 1) % 6
                    if jl == 2:
                        for jl2, j2 in enumerate(js):
                            P.op("pool", lambda e, jl2=jl2, j2=j2: e.dma_start(out=wfo[jl2], in_=w_ffn_out[j2 * 128:(j2 + 1) * 128, :]),
                                 writes=[f"wfo{jl2}"], dma=f"wfo{jl2}")
                        for jl2, j2 in enumerate(js):
                            P.op("pool", lambda e, jl2=jl2: e.tensor_tensor(out=wfo[jl2], in0=wfo[jl2], in1=GA2[:], op=ALU.mult),
                                 reads=["GA2"], writes=[f"wfo{jl2}"])
                    for tt in range(4):
                        bp = pair_ctr[0] % 4
                        pair_ctr[0] += 1
                        bg, bu = 2 * bp, 2 * bp + 1
                        for (s_, b_) in ((sg_, bg), (su_, bu)):
                            for k in range(8):
                                P.op("pe", lambda e, k=k, s_=s_, b_=b_, tt=tt: e.matmul(bank(b_), lhsT=ws[s_][:, k, :], rhs=h[:, k, tt * 512:(tt + 1) * 512],
                                                                                         start=(k == 0), stop=(k == 7)),
                                     reads=[f"ws{s_}"] + hres(k, tt * 512, 512), writes=[f"ps{b_}"], sig=(k == 7))
                        si_ = pair_ctr[0] % 2
                        P.op("act", lambda e, si_=si_, bg=bg: e.activation(out=sgt[:, si_, :], in_=bank(bg), func=AF.Silu),
                             reads=[f"ps{bg}"], writes=[f"sgt{si_}"])
                        P.op("dve", lambda e, si_=si_, bu=bu, jl=jl, tt=tt: e.tensor_tensor(out=aT[:, jl, tt * 512:(tt + 1) * 512], in0=bank(bu),
                                                                                             in1=sgt[:, si_, :], op=ALU.mult),
                             reads=[f"sgt{si_}", f"ps{bu}"], writes=[f"aT{jl}_{tt}"])
                last = (G == len(FGROUPS) - 1)
                nj = len(js)

                def f_mm(n, nj=nj):
                    b0 = 2 * ((pair_base + n) % 4)
                    for jl in range(nj):
                        for hf in range(2):
                            P.op("pe", lambda e, jl=jl, hf=hf: e.matmul(bank(b0 + hf), lhsT=aT[:, jl, n * 128:(n + 1) * 128],
                                                                         rhs=wfo[jl][:, hf * 512:(hf + 1) * 512],
                                                                         start=(jl == 0), stop=(jl == nj - 1)),
                                 reads=[f"aT{jl}_{n // 4}", f"wfo{jl}"], writes=[f"ps{b0 + hf}"], sig=(jl == nj - 1 and hf == 1))

                def f_resid(n):
                    b0 = 2 * ((pair_base + n) % 4)
                    P.op("dve", lambda e: e.tensor_tensor(out=x1[:, n, :], in0=bank(b0, 2), in1=x1[:, n, :], op=ALU.add),
                         reads=[f"ps{b0}", f"ps{b0 + 1}"], writes=[f"x1_{n}"])

                def f_stats(n):
                    si = 40 + n
                    junk2 = xn[:].rearrange("p a b -> p (a b)")[:, 0:1024]
                    P.op("act", lambda e: e.activation(out=junk2, in_=x1[:, n, :], func=AF.Square, accum_out=ss[:, si:si + 1]),
                         reads=[f"x1_{n}"], writes=[f"ss{si}"])
                    P.op("act", lambda e: e.activation(out=sq[:, si:si + 1], in_=ss[:, si:si + 1], func=AF.Sqrt,
                                                       bias=eps_col[:], scale=1.0 / D),
                         reads=[f"ss{si}", "eps_col"], writes=[f"sq{si}"])

                def f_out(n):
                    si = 40 + n
                    P.op("dve", lambda e: e.reciprocal(out=rstd[:, si:si + 1], in_=sq[:, si:si + 1]),
                         reads=[f"sq{si}"], writes=[f"rstd{si}"])
                    P.op("dve", lambda e: e.scalar_tensor_tensor(out=x1[:, n, :], in0=x1[:, n, :], scalar=rstd[:, si:si + 1],
                                                                 in1=GF[:], op0=ALU.mult, op1=ALU.mult),
                         reads=[f"rstd{si}", "GF"], writes=[f"x1_{n}"])
                    P.op("sp", lambda e: e.dma_start(out=out_d[n * 128:(n + 1) * 128, :], in_=x1[:, n, :]),
                         reads=[f"x1_{n}"], writes=[f"out{n}"], dma=f"od{n % 4}")

                pair_base = pair_ctr[0]
                pair_ctr[0] += NB
                for i in range(NB + 3):
                    if 0 <= i < NB:
                        f_mm(i)
                    if 0 <= i - 1 < NB:
                        f_resid(i - 1)
                    if last and 0 <= i - 2 < NB:
                        f_stats(i - 2)
                    if last and 0 <= i - 3 < NB:
                        f_out(i - 3)

            final_waits = [(f"od{i}", P.dma_cnt[f"od{i}"]) for i in range(4)]
        except StopEmit:
            pass


        sem_names = list(Prog.ENGS) + sorted(P.dma_cnt.keys())
        sems = {}
        for s in sem_names:
            sems[s] = es.enter_context(nc.semaphore("s_" + s))
        block = es.enter_context(nc.Block())

        @block.tensor
        def _(eng):
            P.replay("pe", eng, sems)

        @block.scalar
        def _(eng):
            P.replay("act", eng, sems)

        @block.vector
        def _(eng):
            P.replay("dve", eng, sems)

        @block.gpsimd
        def _(eng):
            P.replay("pool", eng, sems)

        @block.sync
        def _(eng):
            P.replay("sp", eng, sems)
            for s, v in final_waits:
                eng.wait_ge(sems[s], v)
    stats = {e: len(P.lists[e]) for e in Prog.ENGS}
    stats["nsems"] = len(sem_names)
    return nc, stats


def _col(vec, nchunks):
    return np.ascontiguousarray(np.asarray(vec, np.float32).reshape(nchunks, 128).T)


def prepare_inputs(x, c, w_ada, b_ada, g_mix, w_in, b_in, sinks, conv_w, w_out, g_ffn, w_ffn_in, w_ffn_out, g_final):
    f32 = np.float32
    x = np.asarray(x, f32); c = np.asarray(c, f32)
    w_ada = np.ascontiguousarray(np.asarray(w_ada, f32)[0]); b_ada = np.asarray(b_ada, f32)[0]
    g_mix = np.asarray(g_mix, f32)[0]; w_in = np.asarray(w_in, f32)[0]; b_in = np.asarray(b_in, f32)[0]
    sinks = np.asarray(sinks, f32)[0]; conv_w = np.asarray(conv_w, f32)[0]
    w_out = np.ascontiguousarray(np.asarray(w_out, f32)[0]); g_ffn = np.asarray(g_ffn, f32)[0]
    w_ffn_in = np.asarray(w_ffn_in, f32)[0]; w_ffn_out = np.ascontiguousarray(np.asarray(w_ffn_out, f32)[0])
    g_final = np.asarray(g_final, f32)

    Q0, K0, V0 = 0, 1024, 1152
    CB0, CC0, CX0, GA0, GC0 = 1280, 2304, 3328, 4352, 5376
    cols = []
    for g in range(2):
        kc = np.arange(K0 + g * 64, K0 + (g + 1) * 64)
        cols.append(np.concatenate([kc, kc]))
    cols.append(np.arange(V0, V0 + 128))
    for ch in range(8):
        r = np.arange(ch * 128, (ch + 1) * 128)
        for base in (CX0, CC0, CB0, GC0, GA0, Q0):
            cols.append(base + r)
    cols = np.stack(cols)
    w_in_k = w_in.reshape(8, 128, -1)
    w_in_perm = np.ascontiguousarray(np.transpose(w_in_k[:, :, cols], (2, 1, 0, 3)).reshape(NZ, 128, 1024))
    b_in_col = np.ascontiguousarray(b_in[cols].T)
    bv_bc = np.ascontiguousarray(np.broadcast_to(b_in[V0:V0 + 128][None, :], (128, 128)))
    fcols = []
    for j in range(NFC):
        fcols.append(np.arange(j * 128, (j + 1) * 128))
        fcols.append(DFF + np.arange(j * 128, (j + 1) * 128))
    fcols = np.stack(fcols)
    wfi_k = w_ffn_in.reshape(8, 128, -1)
    wfi_perm = np.ascontiguousarray(np.transpose(wfi_k[:, :, fcols], (2, 1, 0, 3)).reshape(2 * NFC, 128, 1024))

    heads = (2 * np.arange(8)[None, :] + (np.arange(128)[:, None] >= 64)).astype(np.int64)
    sink_col = np.ascontiguousarray(sinks[heads])
    convw_col = np.ascontiguousarray(np.concatenate([_col(conv_w[k], 8) for k in range(3)], axis=1))
    kk = np.arange(128)[:, None]; qq = np.arange(128)[None, :]
    tri_cur = (kk <= qq).astype(f32)
    tri_prev = (kk > qq).astype(f32)
    mask_cp = np.ascontiguousarray(np.concatenate([tri_cur, tri_prev], axis=1))
    b_ada_col = np.ascontiguousarray(b_ada.reshape(48, 128).T)
    shared = dict(
        w_ada=w_ada, b_ada_col=b_ada_col, gmix_col=_col(g_mix, 8), gffn_col=_col(g_ffn, 8),
        w_in_perm=w_in_perm, b_in_col=b_in_col, bv_bc=bv_bc, sink_col=sink_col, convw_col=convw_col,
        mask_cp=mask_cp, ident=np.eye(128, dtype=f32), w_out=w_out,
        ga1_b_bc=np.ascontiguousarray(np.broadcast_to(b_ada[2048:3072][None, :], (128, D))),
        ga2_b_bc=np.ascontiguousarray(np.broadcast_to(b_ada[5120:6144][None, :], (128, D))),
        gf_bc=np.ascontiguousarray(np.broadcast_to(g_final[None, :], (128, D))),
        wfi_perm=wfi_perm, w_ffn_out=w_ffn_out,
    )
    in_maps = []
    for i in range(NCORES):
        b, qtr = i // 4, i % 4
        s = qtr * NT
        xh = np.zeros((NTH, D), f32)
        xh[128:] = x[b, s:s + NT]
        first = (qtr == 0)
        if not first:
            xh[:128] = x[b, s - 128:s]
        m = dict(shared)
        m["xh"] = xh
        m["c_col"] = _col(c[b], 8)
        m["halo_flag"] = np.full((128, 1), 0.0 if first else 1.0, f32)
        m["mask_halo"] = np.zeros((128, 128), f32) if first else tri_prev.copy()
        in_maps.append(m)
    return in_maps


_CACHE = {}


def kernel(x, c, w_ada, b_ada, g_mix, w_in, b_in, sinks, conv_w, w_out, g_ffn, w_ffn_in, w_ffn_out, g_final):
    in_maps = prepare_inputs(x, c, w_ada, b_ada, g_mix, w_in, b_in, sinks, conv_w, w_out, g_ffn, w_ffn_in, w_ffn_out, g_final)
    if "nc" not in _CACHE:
        _CACHE["nc"] = build_program()[0]
    nc = _CACHE["nc"]
    res = run_bass_kernel_spmd(nc, in_maps, core_ids=list(range(NCORES)))
    out = np.empty((2, 4 * NT, D), np.float32)
    for i in range(NCORES):
        out[i // 4, (i % 4) * NT:(i % 4 + 1) * NT] = res.results[i]["out"]
    return out
```

```python
import contextlib
import numpy as np
import concourse.bass as bass
import concourse.mybir as mybir
from concourse.bass_utils import run_bass_kernel_spmd

F32 = mybir.dt.float32
BF16 = mybir.dt.bfloat16
AF = mybir.ActivationFunctionType
ALU = mybir.AluOpType

D = 1024
NT = 2048
NB = 16
NTH = NT + 128
DFF = 2816
NFC = 22
FGROUPS = [list(range(0, 8)), list(range(8, 16)), list(range(16, 22))]
NZ = 51
EPS = 1e-6
NCORES = 8


class StopEmit(Exception):
    pass


STAGE = 9
DO_ATTN = True
DEFER_TAPS = True
MASK_ENG = 'pool'
PV_LAG = 3
TAIL_DB = True
PV_MERGE = True
TAPS_ON_ACT = True
NPT = 4
NPROJ_BANKS = 4
LATE_AT_B = False
ACT_OEVAC = True
DO_PV = True
DO_NORM = True
DO_DEN = True
DO_HALO = True
NCHUNK_DBG = 8


def checkpoint(n):
    if STAGE < n:
        raise StopEmit()


class Ev:
    __slots__ = ("sem", "value")

    def __init__(self, sem, value):
        self.sem = sem
        self.value = value


class Prog:
    ENGS = ("pe", "act", "dve", "pool", "sp")

    def __init__(self):
        self.lists = {e: [] for e in self.ENGS}
        self.cnt = {e: 0 for e in self.ENGS}
        self.pending = {e: [] for e in self.ENGS}
        self.lastw = {}
        self.readers = {}
        self.dma_cnt = {}
        self.last_ev = {}

    def op(self, eng, fn, reads=(), writes=(), sig=True, dma=None, serialize=True):
        waits = []
        for r in reads:
            if r in self.lastw:
                waits.append((self.lastw[r], "raw"))
        for w in writes:
            if w in self.lastw:
                waits.append((self.lastw[w], "waw"))
            for ev in self.readers.get(w, {}).values():
                waits.append((ev, "war"))
        if dma is not None:
            sres = "sem:" + dma
            if serialize and sres in self.lastw:
                waits.append((self.lastw[sres], "waw"))
            self.dma_cnt[dma] = self.dma_cnt.get(dma, 0) + 16
            ev = Ev(dma, self.dma_cnt[dma])
            self.lastw[sres] = ev
        else:
            ev = Ev(eng, None)
            self.pending[eng].append(ev)
            if sig:
                self.cnt[eng] += 1
                for p in self.pending[eng]:
                    p.value = self.cnt[eng]
                self.pending[eng] = []
            self.last_ev[eng] = ev
        for r in reads:
            self.readers.setdefault(r, {})[ev.sem] = ev
        for w in writes:
            self.lastw[w] = ev
            self.readers[w] = {}
        self.lists[eng].append((fn, waits, sig, dma))
        return ev

    def fence(self, resources, engines=("pe", "act", "dve")):
        for r in resources:
            d = self.readers.setdefault(r, {})
            for e in engines:
                if e in self.last_ev:
                    d[e] = self.last_ev[e]

    def replay(self, eng, engine, sems):
        waited = {}
        for fn, waits, sig, dma in self.lists[eng]:
            need = {}
            for ev, kind in waits:
                if ev.value is None:
                    raise RuntimeError("unresolved event on " + ev.sem)
                if ev.sem == eng:
                    if eng == "pe" or kind == "war":
                        continue
                need[ev.sem] = max(need.get(ev.sem, 0), ev.value)
            for s, v in need.items():
                if waited.get(s, 0) >= v:
                    continue
                engine.wait_ge(sems[s], v)
                waited[s] = v
            ins = fn(engine)
            if dma is not None:
                ins.then_inc(sems[dma], 16)
            elif sig:
                ins.then_inc(sems[eng], 1)


def build_program():
    nc = bass.Bass("TRN2", target_bir_lowering=False)
    P = Prog()

    def din(name, shape):
        return nc.dram_tensor(name, list(shape), F32, kind="ExternalInput").ap()

    xh = din("xh", [NTH, D])
    c_col_d = din("c_col", [128, 8])
    w_ada = din("w_ada", [D, 6 * D])
    b_ada_col_d = din("b_ada_col", [128, 48])
    gmix_col_d = din("gmix_col", [128, 8])
    gffn_col_d = din("gffn_col", [128, 8])
    w_in_perm = din("w_in_perm", [NZ, 128, 1024])
    b_in_col_d = din("b_in_col", [128, NZ])
    bv_bc_d = din("bv_bc", [128, 128])
    sink_col_d = din("sink_col", [128, 8])
    convw_col_d = din("convw_col", [128, 24])
    halo_flag_d = din("halo_flag", [128, 1])
    mask_cp_d = din("mask_cp", [128, 256])
    mask_halo_d = din("mask_halo", [128, 128])
    ident_d = din("ident", [128, 128])
    w_out_d = din("w_out", [D, D])
    ga1_b_d = din("ga1_b_bc", [128, D])
    ga2_b_d = din("ga2_b_bc", [128, D])
    gf_bc_d = din("gf_bc", [128, D])
    wfi_perm = din("wfi_perm", [2 * NFC, 128, 1024])
    w_ffn_out = din("w_ffn_out", [DFF, D])
    out_d = nc.dram_tensor("out", [NT, D], F32, kind="ExternalOutput").ap()

    es = contextlib.ExitStack()

    def sb(name, shape, dt):
        return es.enter_context(nc.sbuf_tensor(name, list(shape), dt))

    with es:
        R_h = sb("R_h", [128, 8 * NTH], BF16)
        R_A = sb("R_A", [128, 32768], BF16)
        R_B = sb("R_B", [128, 24576 + 512], BF16)
        R_W = sb("R_W", [128, 14336], BF16)
        xb = sb("xb", [128, 2, 1024], F32)
        xn = sb("xn", [128, 2, 1024], BF16)
        GA1 = sb("GA1", [128, D], F32)
        GA2 = sb("GA2", [128, D], F32)
        GF = sb("GF", [128, D], F32)
        ident = sb("ident_bf", [128, 128], BF16)
        mask_cp = sb("mask_cp_bf", [128, 256], BF16)
        mask_halo = sb("mask_halo_bf", [128, 128], BF16)
        ones_bf = sb("ones_bf", [128, 128], BF16)
        c_col = sb("c_col_sb", [128, 8], F32)
        sc_bf = sb("sc_bf", [128, 8], BF16)
        sc_f = sb("sc_f", [128, 8], F32)
        sc_rep = sb("sc_rep", [128, 8, 128], BF16)
        b_ada_col = sb("b_ada_col_sb", [128, 48], F32)
        gmix_col = sb("gmix_col_sb", [128, 8], F32)
        gffn_col = sb("gffn_col_sb", [128, 8], F32)
        b_in_col = sb("b_in_col_sb", [128, NZ], F32)
        hb_col = sb("hb_col", [128, NZ], F32)
        bv_bc = sb("bv_bc_sb", [128, 128], F32)
        sink_col = sb("sink_col_sb", [128, 8], F32)
        esink = sb("esink", [128, 8], F32)
        convw = sb("convw_sb", [128, 24], F32)
        halo_flag = sb("halo_flag_sb", [128, 1], F32)
        modcol = sb("modcol", [128, 48], F32)
        a1col = sb("a1col", [128, 8], F32)
        a2col = sb("a2col", [128, 8], F32)
        eps_col = sb("eps_col", [128, 1], F32)
        ss = sb("ss", [128, 64], F32)
        sq = sb("sq", [128, 64], F32)
        rstd = sb("rstd", [128, 64], F32)
        small = sb("small", [128, 16], F32)
        sgt = sb("sgt", [128, 2, 512], BF16)
        psum = es.enter_context(nc.psum_tensor("psum", [128, 4096], F32))

        h = R_h[:].rearrange("p (k t) -> p k t", k=8)
        RA32 = R_A[:].bitcast(F32)
        x1 = RA32.rearrange("p (n d) -> p n d", d=1024)
        def st8(ap):
            return ap.rearrange("p (k n) -> p k n", k=8)
        wada_st = [st8(R_A[:, 0:8192]), st8(R_A[:, 8192:16384]), st8(R_A[:, 21520:29712]),
                   st8(R_W[:, 6144:14336]), st8(R_B[:, 0:8192]), st8(R_B[:, 8192:16384])]
        cx_t = RA32[:, 0:2048]
        u_t = RA32[:, 2048:4104]
        cva_t = [RA32[:, 4104:6152], RA32[:, 10760:12808]]
        ta_t = [RA32[:, 6152:8200], RA32[:, 12808:14856]]
        qT = [R_A[:, 16400:18448], R_A[:, 29712:31760]]
        rden_t = [RA32[:, 9224:9480], RA32[:, 9480:9736]]
        nrm_t = [RA32[:, 9736:9992], RA32[:, 9992:10248]]
        mergedT = R_B[:, 0:16384].rearrange("p (k t) -> p k t", k=8)
        aT = R_B[:, 0:16384].rearrange("p (k t) -> p k t", k=8)
        kT = R_B[:, 16384:16384 + 2 * NTH].rearrange("p (g t) -> p g t", g=2)
        o1 = 16384 + 2 * NTH
        v_sb = R_B[:, o1:o1 + 17 * 128].rearrange("p (b n) -> p b n", n=128)
        o2 = o1 + 17 * 128
        pT = [R_B[:, o2 + i * 512:o2 + (i + 1) * 512].rearrange("p (h q) -> p h q", h=2) for i in range(NPT)]
        wfo = [R_B[:, 16384 + i * 1024:16384 + (i + 1) * 1024] for i in range(8)]
        ws = [R_W[:, i * 1024:(i + 1) * 1024].rearrange("p (k n) -> p k n", k=8) for i in range(6)]
        wout = R_W[:, 6144:14336].rearrange("p (k n) -> p k n", k=8)

        xq = R_B[:, 16384:20480].bitcast(F32)
        xslot = [xb[:, 0, :], xb[:, 1, :], xq[:, 0:1024], xq[:, 1024:2048]]
        mask3 = mask_cp[:].rearrange("p (o q) -> p o q", o=1)
        mask_halo3 = mask_halo[:].rearrange("p (o q) -> p o q", o=1)

        def bank(b, n=1):
            return psum[:, b * 512:(b + n) * 512]

        def bank_bf(b):
            return psum[:, b * 512:(b + 1) * 512].bitcast(BF16).rearrange("p (k t) -> p k t", k=8)

        def hres(k, tok0, ntok):
            return [f"h{k}_{t}" for t in range(tok0 // 128, (tok0 + ntok + 127) // 128)]

        final_waits = []
        try:
            const_res = []

            def cload(eng, dst, src, res):
                P.op(eng, lambda e, dst=dst, src=src: e.dma_start(out=dst, in_=src), writes=[res], dma="const_" + eng, serialize=False)
                const_res.append((res, "const_" + eng))

            cload("sp", c_col[:], c_col_d, "c_col")
            cload("sp", b_ada_col[:], b_ada_col_d, "b_ada_col")
            cload("sp", gmix_col[:], gmix_col_d, "gmix_col")
            cload("sp", gffn_col[:], gffn_col_d, "gffn_col")
            cload("sp", b_in_col[:], b_in_col_d, "b_in_col")
            cload("sp", bv_bc[:], bv_bc_d, "bv_bc")
            cload("sp", sink_col[:], sink_col_d, "sink_col")
            cload("sp", convw[:], convw_col_d, "convw")
            cload("sp", halo_flag[:], halo_flag_d, "halo_flag")
            cload("sp", GA1[:], ga1_b_d, "GA1")
            cload("sp", GA2[:], ga2_b_d, "GA2")
            cload("sp", GF[:], gf_bc_d, "GF")
            cload("pool", ident[:], ident_d, "ident")
            cload("pool", mask_cp[:], mask_cp_d, "mask_cp")
            cload("pool", mask_halo[:], mask_halo_d, "mask_halo")
            for res, s in const_res:
                P.lastw[res] = Ev(s, P.dma_cnt[s])

            wada_v = w_ada.rearrange("(k p) n -> p k n", p=128)
            def load_wada(v, after=()):
                for hf in range(2):
                    P.op("pool", lambda e, hf=hf: e.dma_start(out=wada_st[v][:, 4 * hf:4 * hf + 4, :],
                                                              in_=wada_v[:, 4 * hf:4 * hf + 4, v * 1024:(v + 1) * 1024]),
                         reads=list(after), writes=[f"wada_st{v}_{hf}"], dma=f"wst{v}_{hf}")

            NZI = NZ + 2 * NFC
            loaded = set()

            def load_z(zi):
                if zi >= NZI or zi in loaded:
                    return
                loaded.add(zi)
                i = zi % 6
                src_ap = w_in_perm[zi] if zi < NZ else wfi_perm[zi - NZ]
                P.op("pool", lambda e: e.dma_start(out=ws[i], in_=src_ap.rearrange("p (k n) -> p k n", k=8)),
                     writes=[f"ws{i}"], dma=f"ws{i}")

            load_wada(0)
            load_wada(1)
            for zi in range(6):
                load_z(zi)

            P.op("dve", lambda e: e.memset(eps_col[:], EPS), writes=["eps_col"])
            P.op("dve", lambda e: e.memset(ones_bf[:], 1.0), writes=["ones_bf"])
            P.op("act", lambda e: e.activation(out=sc_f[:], in_=c_col[:], func=AF.Silu), reads=["c_col"], writes=["sc_f"])
            P.op("dve", lambda e: e.tensor_copy(out=sc_bf[:], in_=sc_f[:]), reads=["sc_f"], writes=["sc_bf"])
            for k in range(8):
                P.op("dve", lambda e, k=k: e.tensor_scalar(out=sc_rep[:, k, :], in0=ones_bf[:], scalar1=sc_f[:, k:k + 1], scalar2=None, op0=ALU.mult),
                     reads=["sc_f", "ones_bf"], writes=["sc_rep"])
            P.op("dve", lambda e: e.tensor_scalar(out=hb_col[:], in0=b_in_col[:], scalar1=0.5, scalar2=None, op0=ALU.mult),
                 reads=["b_in_col"], writes=["hb_col"])
            P.op("act", lambda e: e.activation(out=esink[:], in_=sink_col[:], func=AF.Exp), reads=["sink_col"], writes=["esink"])

            def mod_columns(v):
                for j in range(8):
                    for k in range(8):
                        P.op("pe", lambda e, j=j, k=k: e.matmul(psum[:, 3 * 512 + v * 8 + j:3 * 512 + v * 8 + j + 1],
                                                                lhsT=wada_st[v][:, k, j * 128:(j + 1) * 128],
                                                                rhs=sc_bf[:, k:k + 1], start=(k == 0), stop=(k == 7)),
                             reads=[f"wada_st{v}_{k // 4}", "sc_bf"], writes=["ps3"], sig=(k == 7 and j == 7))
                P.op("dve", lambda e: e.tensor_tensor(out=modcol[:, v * 8:v * 8 + 8], in0=psum[:, 3 * 512 + v * 8:3 * 512 + v * 8 + 8],
                                                      in1=b_ada_col[:, v * 8:v * 8 + 8], op=ALU.add),
                     reads=["b_ada_col", "ps3"], writes=[f"modcol{v}"])

            def mod_bcast(v, GA, gres, scale):
                for hf in range(2):
                    for k in range(8):
                        P.op("pe", lambda e, hf=hf, k=k: e.matmul(bank(hf), lhsT=sc_rep[:, k, :], rhs=wada_st[v][:, k, hf * 512:(hf + 1) * 512],
                                                                  start=(k == 0), stop=(k == 7)),
                             reads=[f"wada_st{v}_{k // 4}", "sc_rep"], writes=[f"ps{hf}"], sig=(k == 7))
                    P.op("dve", lambda e, hf=hf: e.tensor_tensor(out=GA[:, hf * 512:(hf + 1) * 512], in0=bank(hf),
                                                                 in1=GA[:, hf * 512:(hf + 1) * 512], op=ALU.add),
                         reads=[gres, f"ps{hf}"], writes=[gres])
                if scale != 1.0:
                    P.op("dve", lambda e: e.tensor_scalar(out=GA[:], in0=GA[:], scalar1=scale, scalar2=None, op0=ALU.mult),
                         reads=[gres], writes=[gres])

            mod_columns(0)
            mod_columns(1)
            P.op("dve", lambda e: e.scalar_tensor_tensor(out=a1col[:], in0=modcol[:, 8:16], scalar=1.0, in1=gmix_col[:],
                                                         op0=ALU.add, op1=ALU.mult),
                 reads=["modcol1", "gmix_col"], writes=["a1col"])

            junk_bf = sgt[:].rearrange("p a b -> p (a b)")

            def nt_stage1(src_ap, src_res, ssi):
                P.op("act", lambda e: e.activation(out=junk_bf, in_=src_ap, func=AF.Square, accum_out=ss[:, ssi:ssi + 1]),
                     reads=[src_res], writes=[f"ss{ssi}"])
                P.op("act", lambda e: e.activation(out=sq[:, ssi:ssi + 1], in_=ss[:, ssi:ssi + 1], func=AF.Sqrt,
                                                   bias=eps_col[:], scale=1.0 / D),
                     reads=[f"ss{ssi}", "eps_col"], writes=[f"sq{ssi}"])

            def nt_recip(ssi):
                P.op("dve", lambda e: e.reciprocal(out=rstd[:, ssi:ssi + 1], in_=sq[:, ssi:ssi + 1]),
                     reads=[f"sq{ssi}"], writes=[f"rstd{ssi}"])

            def nt_copy(src_ap, src_res, ssi):
                xi = ssi % 2
                P.op("act", lambda e: e.activation(out=xn[:, xi, :], in_=src_ap, func=AF.Copy, scale=rstd[:, ssi:ssi + 1]),
                     reads=[src_res, f"rstd{ssi}"], writes=[f"xn{xi}"])

            def nt_transposes(ssi, pbank):
                xi = ssi % 2
                pb = bank_bf(pbank)
                for k in range(8):
                    P.op("pe", lambda e, k=k: e.transpose(pb[:, k, :], xn[:, xi, k * 128:(k + 1) * 128], ident[:]),
                         reads=[f"xn{xi}", "ident"], writes=[f"ps{pbank}"], sig=(k == 7))

            def nt_stage2(src_ap, src_res, ssi, pbank):
                nt_recip(ssi)
                nt_copy(src_ap, src_res, ssi)
                nt_transposes(ssi, pbank)

            def nt_stage3(t_h, acol, shcol, acol_res, shcol_res, pbank):
                pb = bank_bf(pbank)
                for k in range(8):
                    P.op("dve", lambda e, k=k: e.tensor_scalar(out=h[:, k, t_h * 128:(t_h + 1) * 128], in0=pb[:, k, :],
                                                               scalar1=acol[:, k:k + 1], scalar2=shcol[:, k:k + 1],
                                                               op0=ALU.mult, op1=ALU.add),
                         reads=[acol_res, shcol_res, f"ps{pbank}"], writes=[f"h{k}_{t_h}"])

            def emit_late_mods():
                mod_columns(3)
                mod_columns(4)
                P.op("dve", lambda e: e.scalar_tensor_tensor(out=a2col[:], in0=modcol[:, 32:40], scalar=1.0, in1=gffn_col[:],
                                                             op0=ALU.add, op1=ALU.mult),
                     reads=["modcol4", "gffn_col"], writes=["a2col"])
                mod_bcast(2, GA1, "GA1", 0.5)
                mod_bcast(5, GA2, "GA2", 1.0)
                tiles = [0, 512, 1024, 1536]
                P.fence([f"cva1_{t}" for t in tiles] + [f"ta1_{t}" for t in tiles] + [f"qT1_{t}" for t in tiles]
                        + [f"mg{c}_{g}" for c in range(8) for g in range(8)] + ["wout0", "wout1"], engines=("pe",))

            checkpoint(1)
            for i in range(17 + 2):
                if i < 17:
                    t = i
                    xi = t % 4
                    P.op("sp", lambda e, t=t, xi=xi: e.dma_start(out=xslot[xi], in_=xh[t * 128:(t + 1) * 128, :]),
                         writes=[f"xb{xi}"], dma=f"xb{xi}")
                    nt_stage1(xslot[xi], f"xb{xi}", t)
                    nt_stage2(xslot[xi], f"xb{xi}", t, 4 + t % 4)
                if 0 <= i - 1 < 17:
                    t = i - 1
                    nt_stage3(t, a1col[:], modcol[:, 0:8], "a1col", "modcol0", 4 + t % 4)
                if i == 9 and LATE_AT_B:
                    emit_late_mods()
            for v in (3, 4, 5, 2):
                load_wada(v, after=[f"xb{i}" for i in range(4)])
            tiles_ = [0, 512, 1024, 1536]
            P.fence([f"cx_{t}" for t in tiles_] + [f"u_{t}" for t in tiles_] + ["u_h"] + [f"cva0_{t}" for t in tiles_]
                    + [f"ta0_{t}" for t in tiles_] + [f"qT0_{t}" for t in tiles_] + [f"rden{i}" for i in range(2)]
                    + [f"nrm{i}" for i in range(2)], engines=("pe",))

            checkpoint(2)
            wout_v = w_out_d.rearrange("(k p) n -> p k n", p=128)

            proj_bank = [0]

            def proj_group(slot, tok0, ntok, evac):
                b = proj_bank[0] % NPROJ_BANKS
                proj_bank[0] += 1
                for k in range(8):
                    P.op("pe", lambda e, k=k: e.matmul(bank(b)[:, 0:ntok], lhsT=ws[slot][:, k, :], rhs=h[:, k, tok0:tok0 + ntok],
                                                       start=(k == 0), stop=(k == 7)),
                         reads=[f"ws{slot}"] + hres(k, tok0, ntok), writes=[f"ps{b}"], sig=(k == 7))
                evac(b)

            for g in range(2):
                slot = g
                for tok0, ntok in [(0, 128), (128, 512), (640, 512), (1152, 512), (1664, 512)]:
                    def ev_k(b, g=g, tok0=tok0, ntok=ntok):
                        P.op("act", lambda e: e.activation(out=kT[:, g, tok0:tok0 + ntok], in_=bank(b)[:, 0:ntok], func=AF.Identity,
                                                           bias=b_in_col[:, g:g + 1]),
                             reads=["b_in_col", f"ps{b}"], writes=[f"kT{g}_{tok0}"])
                    proj_group(slot, tok0, ntok, ev_k)
                load_z(6 + g)

            def kres(g, tb):
                tok = tb * 128
                for tok0, ntok in [(0, 128), (128, 512), (640, 512), (1152, 512), (1664, 512)]:
                    if tok0 <= tok < tok0 + ntok:
                        return f"kT{g}_{tok0}"

            vslot = 2
            for tb in range(17):
                for k in range(8):
                    P.op("pe", lambda e, tb=tb, k=k: e.matmul(bank(3)[:, 0:128], lhsT=h[:, k, tb * 128:(tb + 1) * 128], rhs=ws[vslot][:, k, :],
                                                              start=(k == 0), stop=(k == 7)),
                         reads=[f"ws{vslot}", f"h{k}_{tb}"], writes=["ps3"], sig=(k == 7))
                P.op("dve", lambda e, tb=tb: e.tensor_tensor(out=v_sb[:, tb, :], in0=bank(3)[:, 0:128], in1=bv_bc[:], op=ALU.add),
                     reads=["bv_bc", "ps3"], writes=[f"v{tb}"])

            load_z(8)
            deferred = []
            tap_q = [[], []]
            checkpoint(3)
            TT = [(0, 512), (512, 512), (1024, 512), (1536, 512)]

            def make_chunk_groups(c):
                groups = []
                par = c % 2
                zb = 3 + 6 * c
                def need_slot(name, zi):
                    load_z(zi)
                    return zi % 6

                def prefetch(t):
                    deferred.append(lambda: load_z(zb + 6 + t))

                for (t0, nt) in TT:
                    def g_cx(t0=t0, nt=nt):
                        s = need_slot("cx", zb + 0)

                        def evac(b):
                            P.op("act", lambda e: e.activation(out=cx_t[:, t0:t0 + nt], in_=bank(b), func=AF.Identity,
                                                               bias=b_in_col[:, zb:zb + 1]),
                                 reads=["b_in_col", f"ps{b}"], writes=[f"cx_{t0}"])
                        proj_group(s, 128 + t0, nt, evac)
                    groups.append(g_cx)

                def g_halo():
                    s_cx = need_slot("cx", zb + 0)
                    s_cc = need_slot("cc", zb + 1)
                    for idx, s in enumerate((s_cx, s_cc)):
                        for k in range(8):
                            P.op("pe", lambda e, k=k, s=s, idx=idx: e.matmul(bank(3)[:, 400 + 2 * idx:402 + 2 * idx], lhsT=ws[s][:, k, :],
                                                                              rhs=h[:, k, 126:128], start=(k == 0), stop=(k == 7)),
                                 reads=[f"ws{s}", f"h{k}_0"], writes=["ps3"], sig=(k == 7))
                    P.op("dve", lambda e: e.tensor_scalar(out=small[:, 0:2], in0=bank(3)[:, 400:402], scalar1=b_in_col[:, zb:zb + 1],
                                                          scalar2=halo_flag[:, 0:1], op0=ALU.add, op1=ALU.mult),
                         reads=["b_in_col", "halo_flag", "ps3"], writes=["small01"])
                    P.op("dve", lambda e: e.scalar_tensor_tensor(out=u_t[:, 0:2], in0=bank(3)[:, 402:404], scalar=b_in_col[:, zb + 1:zb + 2],
                                                                 in1=small[:, 0:2], op0=ALU.add, op1=ALU.mult),
                         reads=["b_in_col", "small01", "ps3"], writes=["u_h"])
                    prefetch(0)
                if DO_HALO:
                    groups.append(g_halo)

                for ti, (t0, nt) in enumerate(TT):
                    def g_cc(t0=t0, nt=nt, ti=ti):
                        s = need_slot("cc", zb + 1)

                        def evac(b):
                            P.op("dve", lambda e: e.scalar_tensor_tensor(out=u_t[:, 2 + t0:2 + t0 + nt], in0=bank(b), scalar=b_in_col[:, zb + 1:zb + 2],
                                                                         in1=cx_t[:, t0:t0 + nt], op0=ALU.add, op1=ALU.mult),
                                 reads=["b_in_col", f"cx_{t0}", f"ps{b}"], writes=[f"u_{t0}"])
                            prev = ["u_h"] if ti == 0 else [f"u_{TT[ti - 1][0]}"]
                            cva = cva_t[par]
                            if DEFER_TAPS:
                                tap_q[0].append(lambda: taps(cva, prev))
                            else:
                                taps(cva, prev)

                        def taps(cva, prev):
                            P.op("act", lambda e: e.activation(out=cva[:, t0:t0 + nt], in_=u_t[:, t0:t0 + nt], func=AF.Copy,
                                                               scale=convw[:, c:c + 1]),
                                 reads=[f"u_{t0}", "convw"] + prev, writes=[f"cva{par}_{t0}"])
                            if TAPS_ON_ACT:
                                j = ti % 2
                                for tap in (1, 2):
                                    tmp = xb[:, j, (tap - 1) * 512:(tap - 1) * 512 + nt]
                                    P.op("act", lambda e, tap=tap, tmp=tmp: e.activation(out=tmp, in_=u_t[:, tap + t0:tap + t0 + nt], func=AF.Copy,
                                                                                         scale=convw[:, 8 * tap + c:8 * tap + c + 1]),
                                         reads=[f"u_{t0}", "convw"] + (prev if tap == 1 else []), writes=[f"tp{j}_{tap}"])
                                    P.op("dve", lambda e, tmp=tmp: e.tensor_tensor(out=cva[:, t0:t0 + nt], in0=cva[:, t0:t0 + nt], in1=tmp, op=ALU.add),
                                         reads=[f"tp{j}_{tap}", f"cva{par}_{t0}"], writes=[f"cva{par}_{t0}"])
                            else:
                                P.op("dve", lambda e: e.scalar_tensor_tensor(out=cva[:, t0:t0 + nt], in0=u_t[:, 1 + t0:1 + t0 + nt],
                                                                             scalar=convw[:, 8 + c:9 + c], in1=cva[:, t0:t0 + nt],
                                                                             op0=ALU.mult, op1=ALU.add),
                                     reads=[f"u_{t0}", "convw", f"cva{par}_{t0}"] + prev, writes=[f"cva{par}_{t0}"])
                                P.op("dve", lambda e: e.scalar_tensor_tensor(out=cva[:, t0:t0 + nt], in0=u_t[:, 2 + t0:2 + t0 + nt],
                                                                             scalar=convw[:, 16 + c:17 + c], in1=cva[:, t0:t0 + nt],
                                                                             op0=ALU.mult, op1=ALU.add),
                                     reads=[f"u_{t0}", "convw", f"cva{par}_{t0}"], writes=[f"cva{par}_{t0}"])
                        proj_group(s, 128 + t0, nt, evac)
                        if ti == 3:
                            prefetch(1)
                    groups.append(g_cc)

                for (t0, nt) in TT:
                    def g_cb(t0=t0, nt=nt):
                        s = need_slot("cb", zb + 2)

                        def evac(b):
                            cva = cva_t[par]
                            P.op("dve", lambda e: e.scalar_tensor_tensor(out=cva[:, t0:t0 + nt], in0=bank(b), scalar=b_in_col[:, zb + 2:zb + 3],
                                                                         in1=cva[:, t0:t0 + nt], op0=ALU.add, op1=ALU.mult),
                                 reads=["b_in_col", f"cva{par}_{t0}", f"ps{b}"], writes=[f"cva{par}_{t0}"])
                        proj_group(s, 128 + t0, nt, evac)
                        if t0 == 1536:
                            prefetch(2)
                    groups.append(g_cb)

                for (t0, nt) in TT:
                    def g_gc(t0=t0, nt=nt):
                        s = need_slot("gc", zb + 3)

                        def evac(b):
                            cva = cva_t[par]
                            P.op("act", lambda e: e.activation(out=cx_t[:, t0:t0 + nt], in_=bank(b), func=AF.Tanh, scale=0.5,
                                                               bias=hb_col[:, zb + 3:zb + 4]),
                                 reads=["hb_col", f"ps{b}"], writes=[f"cx_{t0}"])
                            P.op("dve", lambda e: e.scalar_tensor_tensor(out=cva[:, t0:t0 + nt], in0=cx_t[:, t0:t0 + nt], scalar=1.0,
                                                                         in1=cva[:, t0:t0 + nt], op0=ALU.add, op1=ALU.mult),
                                 reads=[f"cx_{t0}", f"cva{par}_{t0}"], writes=[f"cva{par}_{t0}"])
                        proj_group(s, 128 + t0, nt, evac)
                        if t0 == 1536:
                            prefetch(3)
                    groups.append(g_gc)

                for (t0, nt) in TT:
                    def g_ga(t0=t0, nt=nt):
                        s = need_slot("ga", zb + 4)

                        def evac(b):
                            P.op("act", lambda e: e.activation(out=ta_t[par][:, t0:t0 + nt], in_=bank(b), func=AF.Tanh, scale=0.5,
                                                               bias=hb_col[:, zb + 4:zb + 5]),
                                 reads=["hb_col", f"ps{b}"], writes=[f"ta{par}_{t0}"])
                        proj_group(s, 128 + t0, nt, evac)
                        if t0 == 1536:
                            prefetch(4)
                    groups.append(g_ga)

                for (t0, nt) in TT:
                    def g_q(t0=t0, nt=nt):
                        s = need_slot("q", zb + 5)

                        def evac(b):
                            P.op("act", lambda e: e.activation(out=qT[par][:, t0:t0 + nt], in_=bank(b), func=AF.Identity,
                                                               bias=b_in_col[:, zb + 5:zb + 6]),
                                 reads=["b_in_col", f"ps{b}"], writes=[f"qT{par}_{t0}"])
                        proj_group(s, 128 + t0, nt, evac)
                        if t0 == 1536:
                            prefetch(5)
                    groups.append(g_q)
                return groups

            pT_ctr = [0]

            def make_attn_steps(c):
                par = c % 2
                g = c // 4
                state = {}

                def scores(kb):
                    slot = pT_ctr[0] % NPT
                    pT_ctr[0] += 1
                    state[kb] = slot
                    parts = ([0] if kb >= 0 else []) + ([1] if kb + 1 <= 15 else [])
                    c0, c1 = parts[0] * 128, parts[-1] * 128 + 128
                    qb0 = kb if kb >= 0 else 0
                    qtok0 = qb0 * 128
                    nq = c1 - c0
                    qres = [f"qT{par}_{((qtok0 + i * 128) // 512) * 512}" for i in range(nq // 128)]
                    sb0 = 2 if (c == 7 and TAIL_DB and kb % 2 == 0) else 4
                    for hd in range(2):
                        r0 = hd * 64
                        P.op("pe", lambda e, hd=hd, r0=r0: e.matmul(bank(sb0 + hd)[:, c0:c1], lhsT=kT[r0:r0 + 64, g, (kb + 1) * 128:(kb + 2) * 128],
                                                                    rhs=qT[par][r0:r0 + 64, qtok0:qtok0 + nq], start=True, stop=True,
                                                                    tile_position=(r0, 0)),
                             reads=[kres(g, kb + 1)] + qres, writes=[f"ps{sb0 + hd}"], sig=True)
                    for hd in range(2):
                        P.op("act", lambda e, hd=hd: e.activation(out=pT[slot][:, hd, c0:c1], in_=bank(sb0 + hd)[:, c0:c1], func=AF.Exp, scale=0.125),
                             reads=[f"ps{sb0 + hd}"], writes=[f"pT{slot}_{hd}"])
                    msk = mask3[:, :, c0:c1] if kb >= 0 else mask_halo3
                    P.op(MASK_ENG, lambda e: e.tensor_tensor(out=pT[slot][:, :, c0:c1], in0=pT[slot][:, :, c0:c1],
                                                             in1=msk.to_broadcast([128, 2, c1 - c0]), op=ALU.mult),
                         reads=["mask_cp", "mask_halo"], writes=[f"pT{slot}_0", f"pT{slot}_1"])

                def pv(kb):
                    slot = state[kb]
                    if PV_MERGE and kb >= 0 and kb % 2 == 0:
                        g2 = kb // 2
                        b = 6 + g2 % 2
                        for (lhs_fn, col0, res) in ((lambda: v_sb[:, kb + 1, g * 64:(g + 1) * 64], 0, f"v{kb + 1}"),
                                                    (lambda: ones_bf[:, 0:64], 256, "ones_bf")):
                            for hd in range(2):
                                r0 = hd * 64
                                P.op("pe", lambda e, hd=hd, r0=r0, lhs_fn=lhs_fn, col0=col0: e.matmul(
                                         bank(b)[r0:r0 + 64, col0:col0 + 256], lhsT=lhs_fn(), rhs=pT[slot][:, hd, 0:256],
                                         start=False, stop=False, skip_group_check=True, tile_position=(0, r0)),
                                     reads=[res, f"pT{slot}_{hd}"], writes=[f"ps{b}"], sig=(col0 == 256 and hd == 1))
                        return
                    parts = ([(0, kb)] if kb >= 0 else []) + ([(1, kb + 1)] if kb + 1 <= 15 else [])
                    for (pi, n) in parts:
                        pv_part(kb, slot, pi, n)

                def pv_part(kb, slot, pi, n):
                    if True:
                        g2 = n // 2
                        b = 6 + g2 % 2
                        ocol = (n % 2) * 128
                        dcol = 256 + ocol
                        first_in_bank = (pi == 1 and n % 2 == 0)
                        for hd in range(2):
                            r0 = hd * 64
                            P.op("pe", lambda e, hd=hd, r0=r0: e.matmul(bank(b)[r0:r0 + 64, ocol:ocol + 128], lhsT=v_sb[:, kb + 1, g * 64:(g + 1) * 64],
                                                                        rhs=pT[slot][:, hd, pi * 128:(pi + 1) * 128], start=first_in_bank, stop=(pi == 0),
                                                                        skip_group_check=True, tile_position=(0, r0)),
                                 reads=[f"v{kb + 1}", f"pT{slot}_{hd}"], writes=[f"ps{b}"], sig=False)
                        for hd in range(2):
                            r0 = hd * 64
                            P.op("pe", lambda e, hd=hd, r0=r0: e.matmul(bank(b)[r0:r0 + 64, dcol:dcol + 128], lhsT=ones_bf[:, 0:64],
                                                                        rhs=pT[slot][:, hd, pi * 128:(pi + 1) * 128], start=False, stop=(pi == 0),
                                                                        skip_group_check=True, tile_position=(0, r0)),
                                 reads=["ones_bf", f"pT{slot}_{hd}"], writes=[f"ps{b}"], sig=(hd == 1))
                        if pi == 0 and n % 2 == 1 and DO_NORM:
                            tok0 = g2 * 256
                            i2 = g2 % 2
                            tt0 = (tok0 // 512) * 512
                            P.op("act", lambda e: e.activation(out=rden_t[i2], in_=bank(b)[:, 256:512], func=AF.Identity,
                                                               bias=esink[:, c:c + 1]),
                                 reads=["esink", f"ps{b}"], writes=[f"rden{i2}"])
                            if ACT_OEVAC:
                                P.op("act", lambda e: e.activation(out=nrm_t[i2], in_=bank(b)[:, 0:256], func=AF.Copy),
                                     reads=[f"ps{b}"], writes=[f"nrm{i2}"])
                            P.op("dve", lambda e: e.reciprocal(out=rden_t[i2], in_=rden_t[i2]), reads=[f"rden{i2}"], writes=[f"rden{i2}"])
                            if ACT_OEVAC:
                                P.op("dve", lambda e: e.tensor_tensor(out=nrm_t[i2], in0=nrm_t[i2], in1=rden_t[i2], op=ALU.mult),
                                     reads=[f"rden{i2}", f"nrm{i2}"], writes=[f"nrm{i2}"])
                            else:
                                P.op("dve", lambda e: e.tensor_tensor(out=nrm_t[i2], in0=bank(b)[:, 0:256], in1=rden_t[i2], op=ALU.mult),
                                     reads=[f"rden{i2}", f"ps{b}"], writes=[f"nrm{i2}"])
                            P.op("dve", lambda e: e.scalar_tensor_tensor(out=nrm_t[i2], in0=ta_t[par][:, tok0:tok0 + 256], scalar=1.0,
                                                                         in1=nrm_t[i2], op0=ALU.add, op1=ALU.mult),
                                 reads=[f"ta{par}_{tt0}", f"nrm{i2}"], writes=[f"nrm{i2}"])
                            P.op("dve", lambda e: e.tensor_tensor(out=mergedT[:, c, tok0:tok0 + 256], in0=nrm_t[i2],
                                                                  in1=cva_t[par][:, tok0:tok0 + 256], op=ALU.add),
                                 reads=[f"nrm{i2}", f"cva{par}_{tt0}"], writes=[f"mg{c}_{g2}"])

                steps = []
                for i in range(-1, 16 + PV_LAG):
                    def st(i=i):
                        if i <= 15:
                            scores(i)
                        if -1 <= i - PV_LAG <= 15 and DO_PV:
                            pv(i - PV_LAG)
                    steps.append(st)
                return steps

            for c in range(9):
                if c > NCHUNK_DBG:
                    break
                groups = make_chunk_groups(c) if c < min(8, NCHUNK_DBG) else []
                steps = make_attn_steps(c - 1) if (c >= 1 and DO_ATTN) else []
                n = max(len(groups), len(steps))
                for i in range(n):
                    old = list(deferred) + tap_q[1]
                    del deferred[:]
                    tap_q[1] = tap_q[0]
                    tap_q[0] = []
                    if i < len(groups):
                        groups[i]()
                    for f in old:
                        f()
                    if i < len(steps):
                        steps[i]()
                    if c == 2 and i % 3 == 0 and i // 3 < 8:
                        kk = i // 3
                        P.op("pool", lambda e, kk=kk: e.tensor_tensor(out=wout[:, kk, :], in0=wout[:, kk, :], in1=GA1[:], op=ALU.mult),
                             reads=["GA1"], writes=[f"wout{kk // 4}"])
                for f in deferred + tap_q[1] + tap_q[0]:
                    f()
                del deferred[:]
                tap_q[0], tap_q[1] = [], []
                if c == 7 and STAGE >= 4:
                    P.fence([f"x1_{n}" for n in range(9)])
                    for n in range(9):
                        P.op("sp", lambda e, n=n: e.dma_start(out=x1[:, n, :], in_=xh[128 + n * 128:128 + (n + 1) * 128, :]),
                             writes=[f"x1_{n}"], dma=f"xr{n % 4}")
                if c == 0:
                    if not LATE_AT_B:
                        emit_late_mods()
                    for hf in range(2):
                        P.op("pool", lambda e, hf=hf: e.dma_start(out=wout[:, 4 * hf:4 * hf + 4, :], in_=wout_v[:, 4 * hf:4 * hf + 4, :]),
                             writes=[f"wout{hf}"], dma=f"wout{hf}")

            sh2col = modcol[:, 24:32]
            sh2res = "modcol3"

            checkpoint(4)
            P.fence([f"x1_{n}" for n in range(9, NB)])
            for n in range(9, NB):
                P.op("sp", lambda e, n=n: e.dma_start(out=x1[:, n, :], in_=xh[128 + n * 128:128 + (n + 1) * 128, :]),
                     writes=[f"x1_{n}"], dma=f"xr{n % 4}")
            def d_mm(n):
                b0 = 2 * (n % 2)
                for k in range(8):
                    for hf in range(2):
                        P.op("pe", lambda e, k=k, hf=hf: e.matmul(bank(b0 + hf), lhsT=mergedT[:, k, n * 128:(n + 1) * 128],
                                                                   rhs=wout[:, k, hf * 512:(hf + 1) * 512], start=(k == 0), stop=(k == 7)),
                             reads=[f"mg{k}_{n // 2}", f"wout{k // 4}"], writes=[f"ps{b0 + hf}"], sig=(k == 7 and hf == 1))

            def d_resid(n):
                b0 = 2 * (n % 2)
                P.op("dve", lambda e: e.tensor_tensor(out=x1[:, n, :], in0=bank(b0, 2), in1=x1[:, n, :], op=ALU.add),
                     reads=[f"ps{b0}", f"ps{b0 + 1}"], writes=[f"x1_{n}"])

            for i in range(NB + 6):
                def ok(j):
                    return 0 <= j < NB
                if ok(i):
                    d_mm(i)
                if ok(i - 1):
                    d_resid(i - 1)
                if ok(i - 2):
                    n = i - 2
                    nt_stage1(x1[:, n, :], f"x1_{n}", 20 + n)
                if ok(i - 3):
                    nt_recip(20 + i - 3)
                if ok(i - 4):
                    n = i - 4
                    nt_copy(x1[:, n, :], f"x1_{n}", 20 + n)
                if ok(i - 5):
                    n = i - 5
                    nt_transposes(20 + n, 4 + n % 4)
                if ok(i - 6):
                    n = i - 6
                    nt_stage3(n, a2col[:], sh2col, "a2col", sh2res, 4 + n % 4)

            checkpoint(5)
            P.fence([f"aT{jl}_{tt}" for jl in range(8) for tt in range(4)] + [f"wfo{i}" for i in range(8)])
            pair_ctr = [0]
            for G, js in enumerate(FGROUPS):
                for jl, j in enumerate(js):
                    load_z(NZ + 2 * j)
                    load_z(NZ + 2 * j + 1)
                    sg_ = (NZ + 2 * j) % 6
                    su_ = (NZ + 2 * j + 1) % 6
                    if jl == 2:
                        for jl2, j2 in enumerate(js):
                            P.op("pool", lambda e, jl2=jl2, j2=j2: e.dma_start(out=wfo[jl2], in_=w_ffn_out[j2 * 128:(j2 + 1) * 128, :]),
                                 writes=[f"wfo{jl2}"], dma=f"wfo{jl2}")
                        for jl2, j2 in enumerate(js):
                            P.op("pool", lambda e, jl2=jl2: e.tensor_tensor(out=wfo[jl2], in0=wfo[jl2], in1=GA2[:], op=ALU.mult),
                                 reads=["GA2"], writes=[f"wfo{jl2}"])
                    for tt in range(4):
                        bp = pair_ctr[0] % 4
                        pair_ctr[0] += 1
                        bg, bu = 2 * bp, 2 * bp + 1
                        for (s_, b_) in ((sg_, bg), (su_, bu)):
                            for k in range(8):
                                P.op("pe", lambda e, k=k, s_=s_, b_=b_, tt=tt: e.matmul(bank(b_), lhsT=ws[s_][:, k, :], rhs=h[:, k, tt * 512:(tt + 1) * 512],
                                                                                         start=(k == 0), stop=(k == 7)),
                                     reads=[f"ws{s_}"] + hres(k, tt * 512, 512), writes=[f"ps{b_}"], sig=(k == 7))
                        si_ = pair_ctr[0] % 2
                        P.op("act", lambda e, si_=si_, bg=bg: e.activation(out=sgt[:, si_, :], in_=bank(bg), func=AF.Silu),
                             reads=[f"ps{bg}"], writes=[f"sgt{si_}"])
                        P.op("dve", lambda e, si_=si_, bu=bu, jl=jl, tt=tt: e.tensor_tensor(out=aT[:, jl, tt * 512:(tt + 1) * 512], in0=bank(bu),
                                                                                             in1=sgt[:, si_, :], op=ALU.mult),
                             reads=[f"sgt{si_}", f"ps{bu}"], writes=[f"aT{jl}_{tt}"])
                last = (G == len(FGROUPS) - 1)
                nj = len(js)

                def f_mm(n, nj=nj):
                    b0 = 2 * ((pair_base + n) % 4)
                    for jl in range(nj):
                        for hf in range(2):
                            P.op("pe", lambda e, jl=jl, hf=hf: e.matmul(bank(b0 + hf), lhsT=aT[:, jl, n * 128:(n + 1) * 128],
                                                                         rhs=wfo[jl][:, hf * 512:(hf + 1) * 512],
                                                                         start=(jl == 0), stop=(jl == nj - 1)),
                                 reads=[f"aT{jl}_{n // 4}", f"wfo{jl}"], writes=[f"ps{b0 + hf}"], sig=(jl == nj - 1 and hf == 1))

                def f_resid(n):
                    b0 = 2 * ((pair_base + n) % 4)
                    P.op("dve", lambda e: e.tensor_tensor(out=x1[:, n, :], in0=bank(b0, 2), in1=x1[:, n, :], op=ALU.add),
                         reads=[f"ps{b0}", f"ps{b0 + 1}"], writes=[f"x1_{n}"])

                def f_stats(n):
                    si = 40 + n
                    junk2 = xn[:].rearrange("p a b -> p (a b)")[:, 0:1024]
                    P.op("act", lambda e: e.activation(out=junk2, in_=x1[:, n, :], func=AF.Square, accum_out=ss[:, si:si + 1]),
                         reads=[f"x1_{n}"], writes=[f"ss{si}"])
                    P.op("act", lambda e: e.activation(out=sq[:, si:si + 1], in_=ss[:, si:si + 1], func=AF.Sqrt,
                                                       bias=eps_col[:], scale=1.0 / D),
                         reads=[f"ss{si}", "eps_col"], writes=[f"sq{si}"])

                def f_out(n):
                    si = 40 + n
                    P.op("dve", lambda e: e.reciprocal(out=rstd[:, si:si + 1], in_=sq[:, si:si + 1]),
                         reads=[f"sq{si}"], writes=[f"rstd{si}"])
                    P.op("dve", lambda e: e.scalar_tensor_tensor(out=x1[:, n, :], in0=x1[:, n, :], scalar=rstd[:, si:si + 1],
                                                                 in1=GF[:], op0=ALU.mult, op1=ALU.mult),
                         reads=[f"rstd{si}", "GF"], writes=[f"x1_{n}"])
                    P.op("sp", lambda e: e.dma_start(out=out_d[n * 128:(n + 1) * 128, :], in_=x1[:, n, :]),
                         reads=[f"x1_{n}"], writes=[f"out{n}"], dma=f"od{n % 4}")

                pair_base = pair_ctr[0]
                pair_ctr[0] += NB
                for i in range(NB + 3):
                    if 0 <= i < NB:
                        f_mm(i)
                    if 0 <= i - 1 < NB:
                        f_resid(i - 1)
                    if last and 0 <= i - 2 < NB:
                        f_stats(i - 2)
                    if last and 0 <= i - 3 < NB:
                        f_out(i - 3)

            final_waits = [(f"od{i}", P.dma_cnt[f"od{i}"]) for i in range(4)]
        except StopEmit:
            pass


        sem_names = list(Prog.ENGS) + sorted(P.dma_cnt.keys())
        sems = {}
        for s in sem_names:
            sems[s] = es.enter_context(nc.semaphore("s_" + s))
        block = es.enter_context(nc.Block())

        @block.tensor
        def _(eng):
            P.replay("pe", eng, sems)

        @block.scalar
        def _(eng):
            P.replay("act", eng, sems)

        @block.vector
        def _(eng):
            P.replay("dve", eng, sems)

        @block.gpsimd
        def _(eng):
            P.replay("pool", eng, sems)

        @block.sync
        def _(eng):
            P.replay("sp", eng, sems)
            for s, v in final_waits:
                eng.wait_ge(sems[s], v)
    stats = {e: len(P.lists[e]) for e in Prog.ENGS}
    stats["nsems"] = len(sem_names)
    return nc, stats


def _col(vec, nchunks):
    return np.ascontiguousarray(np.asarray(vec, np.float32).reshape(nchunks, 128).T)


def prepare_inputs(x, c, w_ada, b_ada, g_mix, w_in, b_in, sinks, conv_w, w_out, g_ffn, w_ffn_in, w_ffn_out, g_final):
    f32 = np.float32
    x = np.asarray(x, f32); c = np.asarray(c, f32)
    w_ada = np.ascontiguousarray(np.asarray(w_ada, f32)[0]); b_ada = np.asarray(b_ada, f32)[0]
    g_mix = np.asarray(g_mix, f32)[0]; w_in = np.asarray(w_in, f32)[0]; b_in = np.asarray(b_in, f32)[0]
    sinks = np.asarray(sinks, f32)[0]; conv_w = np.asarray(conv_w, f32)[0]
    w_out = np.ascontiguousarray(np.asarray(w_out, f32)[0]); g_ffn = np.asarray(g_ffn, f32)[0]
    w_ffn_in = np.asarray(w_ffn_in, f32)[0]; w_ffn_out = np.ascontiguousarray(np.asarray(w_ffn_out, f32)[0])
    g_final = np.asarray(g_final, f32)

    Q0, K0, V0 = 0, 1024, 1152
    CB0, CC0, CX0, GA0, GC0 = 1280, 2304, 3328, 4352, 5376
    cols = []
    for g in range(2):
        kc = np.arange(K0 + g * 64, K0 + (g + 1) * 64)
        cols.append(np.concatenate([kc, kc]))
    cols.append(np.arange(V0, V0 + 128))
    for ch in range(8):
        r = np.arange(ch * 128, (ch + 1) * 128)
        for base in (CX0, CC0, CB0, GC0, GA0, Q0):
            cols.append(base + r)
    cols = np.stack(cols)
    w_in_k = w_in.reshape(8, 128, -1)
    w_in_perm = np.ascontiguousarray(np.transpose(w_in_k[:, :, cols], (2, 1, 0, 3)).reshape(NZ, 128, 1024))
    b_in_col = np.ascontiguousarray(b_in[cols].T)
    bv_bc = np.ascontiguousarray(np.broadcast_to(b_in[V0:V0 + 128][None, :], (128, 128)))
    fcols = []
    for j in range(NFC):
        fcols.append(np.arange(j * 128, (j + 1) * 128))
        fcols.append(DFF + np.arange(j * 128, (j + 1) * 128))
    fcols = np.stack(fcols)
    wfi_k = w_ffn_in.reshape(8, 128, -1)
    wfi_perm = np.ascontiguousarray(np.transpose(wfi_k[:, :, fcols], (2, 1, 0, 3)).reshape(2 * NFC, 128, 1024))

    heads = (2 * np.arange(8)[None, :] + (np.arange(128)[:, None] >= 64)).astype(np.int64)
    sink_col = np.ascontiguousarray(sinks[heads])
    convw_col = np.ascontiguousarray(np.concatenate([_col(conv_w[k], 8) for k in range(3)], axis=1))
    kk = np.arange(128)[:, None]; qq = np.arange(128)[None, :]
    tri_cur = (kk <= qq).astype(f32)
    tri_prev = (kk > qq).astype(f32)
    mask_cp = np.ascontiguousarray(np.concatenate([tri_cur, tri_prev], axis=1))
    b_ada_col = np.ascontiguousarray(b_ada.reshape(48, 128).T)
    shared = dict(
        w_ada=w_ada, b_ada_col=b_ada_col, gmix_col=_col(g_mix, 8), gffn_col=_col(g_ffn, 8),
        w_in_perm=w_in_perm, b_in_col=b_in_col, bv_bc=bv_bc, sink_col=sink_col, convw_col=convw_col,
        mask_cp=mask_cp, ident=np.eye(128, dtype=f32), w_out=w_out,
        ga1_b_bc=np.ascontiguousarray(np.broadcast_to(b_ada[2048:3072][None, :], (128, D))),
        ga2_b_bc=np.ascontiguousarray(np.broadcast_to(b_ada[5120:6144][None, :], (128, D))),
        gf_bc=np.ascontiguousarray(np.broadcast_to(g_final[None, :], (128, D))),
        wfi_perm=wfi_perm, w_ffn_out=w_ffn_out,
    )
    in_maps = []
    for i in range(NCORES):
        b, qtr = i // 4, i % 4
        s = qtr * NT
        xh = np.zeros((NTH, D), f32)
        xh[128:] = x[b, s:s + NT]
        first = (qtr == 0)
        if not first:
            xh[:128] = x[b, s - 128:s]
        m = dict(shared)
        m["xh"] = xh
        m["c_col"] = _col(c[b], 8)
        m["halo_flag"] = np.full((128, 1), 0.0 if first else 1.0, f32)
        m["mask_halo"] = np.zeros((128, 128), f32) if first else tri_prev.copy()
        in_maps.append(m)
    return in_maps


_CACHE = {}


def kernel(x, c, w_ada, b_ada, g_mix, w_in, b_in, sinks, conv_w, w_out, g_ffn, w_ffn_in, w_ffn_out, g_final):
    in_maps = prepare_inputs(x, c, w_ada, b_ada, g_mix, w_in, b_in, sinks, conv_w, w_out, g_ffn, w_ffn_in, w_ffn_out, g_final)
    if "nc" not in _CACHE:
        _CACHE["nc"] = build_program()[0]
    nc = _CACHE["nc"]
    res = run_bass_kernel_spmd(nc, in_maps, core_ids=list(range(NCORES)))
    out = np.empty((2, 4 * NT, D), np.float32)
    for i in range(NCORES):
        out[i // 4, (i % 4) * NT:(i % 4 + 1) * NT] = res.results[i]["out"]
    return out
```

```python
import contextlib
import numpy as np
import concourse.bass as bass
import concourse.mybir as mybir
from concourse.bass_utils import run_bass_kernel_spmd

F32 = mybir.dt.float32
BF16 = mybir.dt.bfloat16
AF = mybir.ActivationFunctionType
ALU = mybir.AluOpType

D = 1024
NT = 2048
NB = 16
NTH = NT + 128
DFF = 2816
NFC = 22
FGROUPS = [list(range(0, 8)), list(range(8, 15)), list(range(15, 22))]
NZ = 51
EPS = 1e-6
NCORES = 8


class StopEmit(Exception):
    pass


STAGE = 9
DO_ATTN = True
DEFER_TAPS = True
MASK_ENG = 'pool'
PV_LAG = 3
TAIL_DB = True
PV_MERGE = True
EXP_MERGE = True
TAPS_ON_ACT = True
NPT = 4
NPROJ_BANKS = 4
LATE_AT_B = False
ACT_OEVAC = True
DO_PV = True
DO_NORM = True
DO_DEN = True
DO_HALO = True
NCHUNK_DBG = 8


def checkpoint(n):
    if STAGE < n:
        raise StopEmit()


class Ev:
    __slots__ = ("sem", "value")

    def __init__(self, sem, value):
        self.sem = sem
        self.value = value


class Prog:
    ENGS = ("pe", "act", "dve", "pool", "sp")

    def __init__(self):
        self.lists = {e: [] for e in self.ENGS}
        self.cnt = {e: 0 for e in self.ENGS}
        self.pending = {e: [] for e in self.ENGS}
        self.lastw = {}
        self.readers = {}
        self.dma_cnt = {}
        self.last_ev = {}

    def op(self, eng, fn, reads=(), writes=(), sig=True, dma=None, serialize=True):
        waits = []
        for r in reads:
            if r in self.lastw:
                waits.append((self.lastw[r], "raw"))
        for w in writes:
            if w in self.lastw:
                waits.append((self.lastw[w], "waw"))
            for ev in self.readers.get(w, {}).values():
                waits.append((ev, "war"))
        if dma is not None:
            sres = "sem:" + dma
            if serialize and sres in self.lastw:
                waits.append((self.lastw[sres], "waw"))
            self.dma_cnt[dma] = self.dma_cnt.get(dma, 0) + 16
            ev = Ev(dma, self.dma_cnt[dma])
            self.lastw[sres] = ev
        else:
            ev = Ev(eng, None)
            self.pending[eng].append(ev)
            if sig:
                self.cnt[eng] += 1
                for p in self.pending[eng]:
                    p.value = self.cnt[eng]
                self.pending[eng] = []
            self.last_ev[eng] = ev
        for r in reads:
            self.readers.setdefault(r, {})[ev.sem] = ev
        for w in writes:
            self.lastw[w] = ev
            self.readers[w] = {}
        self.lists[eng].append((fn, waits, sig, dma))
        return ev

    def fence(self, resources, engines=("pe", "act", "dve", "pool")):
        for r in resources:
            d = self.readers.setdefault(r, {})
            for e in engines:
                if e in self.last_ev:
                    d[e] = self.last_ev[e]

    def replay(self, eng, engine, sems):
        waited = {}
        for fn, waits, sig, dma in self.lists[eng]:
            need = {}
            for ev, kind in waits:
                if ev.value is None:
                    raise RuntimeError("unresolved event on " + ev.sem)
                if ev.sem == eng:
                    if eng == "pe" or kind == "war":
                        continue
                need[ev.sem] = max(need.get(ev.sem, 0), ev.value)
            for s, v in need.items():
                if waited.get(s, 0) >= v:
                    continue
                engine.wait_ge(sems[s], v)
                waited[s] = v
            ins = fn(engine)
            if dma is not None:
                ins.then_inc(sems[dma], 16)
            elif sig:
                ins.then_inc(sems[eng], 1)


def build_program():
    nc = bass.Bass("TRN2", target_bir_lowering=False)
    P = Prog()

    def din(name, shape):
        return nc.dram_tensor(name, list(shape), F32, kind="ExternalInput").ap()

    xh = din("xh", [NTH, D])
    c_col_d = din("c_col", [128, 8])
    w_ada = din("w_ada", [D, 6 * D])
    b_ada_col_d = din("b_ada_col", [128, 48])
    gmix_col_d = din("gmix_col", [128, 8])
    gffn_col_d = din("gffn_col", [128, 8])
    w_in_perm = din("w_in_perm", [NZ, 128, 1024])
    b_in_col_d = din("b_in_col", [128, NZ])
    bv_bc_d = din("bv_bc", [128, 128])
    sink_col_d = din("sink_col", [128, 8])
    convw_col_d = din("convw_col", [128, 24])
    halo_flag_d = din("halo_flag", [128, 1])
    mask_cp_d = din("mask_cp", [128, 256])
    mask_halo_d = din("mask_halo", [128, 128])
    ident_d = din("ident", [128, 128])
    w_out_d = din("w_out", [D, D])
    ga1_b_d = din("ga1_b_bc", [128, D])
    ga2_b_d = din("ga2_b_bc", [128, D])
    gf_bc_d = din("gf_bc", [128, D])
    wfi_perm = din("wfi_perm", [2 * NFC, 128, 1024])
    w_ffn_out = din("w_ffn_out", [DFF, D])
    out_d = nc.dram_tensor("out", [NT, D], F32, kind="ExternalOutput").ap()

    es = contextlib.ExitStack()

    def sb(name, shape, dt):
        return es.enter_context(nc.sbuf_tensor(name, list(shape), dt))

    with es:
        R_h = sb("R_h", [128, 8 * NTH], BF16)
        R_A = sb("R_A", [128, 32768], BF16)
        R_B = sb("R_B", [128, 24576 + 512], BF16)
        R_W = sb("R_W", [128, 14336], BF16)
        xb = sb("xb", [128, 2, 1024], F32)
        xn = sb("xn", [128, 2, 1024], BF16)
        GA1 = sb("GA1", [128, D], F32)
        GA2 = sb("GA2", [128, D], F32)
        GF = sb("GF", [128, D], F32)
        ident = sb("ident_bf", [128, 128], BF16)
        mask_cp = sb("mask_cp_bf", [128, 256], BF16)
        mask_halo = sb("mask_halo_bf", [128, 128], BF16)
        ones_bf = sb("ones_bf", [128, 128], BF16)
        c_col = sb("c_col_sb", [128, 8], F32)
        sc_bf = sb("sc_bf", [128, 8], BF16)
        sc_f = sb("sc_f", [128, 8], F32)
        sc_rep = sb("sc_rep", [128, 8, 128], BF16)
        b_ada_col = sb("b_ada_col_sb", [128, 48], F32)
        gmix_col = sb("gmix_col_sb", [128, 8], F32)
        gffn_col = sb("gffn_col_sb", [128, 8], F32)
        b_in_col = sb("b_in_col_sb", [128, NZ], F32)
        hb_col = sb("hb_col", [128, NZ], F32)
        bv_bc = sb("bv_bc_sb", [128, 128], F32)
        sink_col = sb("sink_col_sb", [128, 8], F32)
        esink = sb("esink", [128, 8], F32)
        convw = sb("convw_sb", [128, 24], F32)
        halo_flag = sb("halo_flag_sb", [128, 1], F32)
        modcol = sb("modcol", [128, 48], F32)
        a1col = sb("a1col", [128, 8], F32)
        a2col = sb("a2col", [128, 8], F32)
        eps_col = sb("eps_col", [128, 1], F32)
        ss = sb("ss", [128, 64], F32)
        sq = sb("sq", [128, 64], F32)
        rstd = sb("rstd", [128, 64], F32)
        small = sb("small", [128, 16], F32)
        sgt = sb("sgt", [128, 2, 512], BF16)
        psum = es.enter_context(nc.psum_tensor("psum", [128, 4096], F32))

        h = R_h[:].rearrange("p (k t) -> p k t", k=8)
        RA32 = R_A[:].bitcast(F32)
        x1 = RA32.rearrange("p (n d) -> p n d", d=1024)
        def st8(ap):
            return ap.rearrange("p (k n) -> p k n", k=8)
        wada_st = [st8(R_A[:, 0:8192]), st8(R_A[:, 8192:16384]), st8(R_A[:, 21520:29712]),
                   st8(R_W[:, 6144:14336]), st8(R_B[:, 0:8192]), st8(R_B[:, 8192:16384])]
        cx_t = RA32[:, 0:2048]
        u_t = RA32[:, 2048:4104]
        cva_t = [RA32[:, 4104:6152], RA32[:, 10760:12808]]
        ta_t = [RA32[:, 6152:8200], RA32[:, 12808:14856]]
        qT = [R_A[:, 16400:18448], R_A[:, 29712:31760]]
        rden_t = [RA32[:, 9224:9480], RA32[:, 9480:9736]]
        nrm_t = [RA32[:, 9736:9992], RA32[:, 9992:10248]]
        mergedT = R_B[:, 0:16384].rearrange("p (k t) -> p k t", k=8)
        aT = R_B[:, 0:16384].rearrange("p (k t) -> p k t", k=8)
        kT = R_B[:, 16384:16384 + 2 * NTH].rearrange("p (g t) -> p g t", g=2)
        o1 = 16384 + 2 * NTH
        v_sb = R_B[:, o1:o1 + 17 * 128].rearrange("p (b n) -> p b n", n=128)
        o2 = o1 + 17 * 128
        pT = [R_B[:, o2 + i * 512:o2 + (i + 1) * 512].rearrange("p (h q) -> p h q", h=2) for i in range(NPT)]
        wfo = [R_B[:, 16384 + i * 1024:16384 + (i + 1) * 1024] for i in range(8)]
        ws = [R_W[:, i * 1024:(i + 1) * 1024].rearrange("p (k n) -> p k n", k=8) for i in range(6)]
        wout = R_W[:, 6144:14336].rearrange("p (k n) -> p k n", k=8)

        xq = R_B[:, 16384:20480].bitcast(F32)
        xslot = [xb[:, 0, :], xb[:, 1, :], xq[:, 0:1024], xq[:, 1024:2048]]
        mask3 = mask_cp[:].rearrange("p (o q) -> p o q", o=1)
        mask_halo3 = mask_halo[:].rearrange("p (o q) -> p o q", o=1)

        def bank(b, n=1):
            return psum[:, b * 512:(b + n) * 512]

        def bank_bf(b):
            return psum[:, b * 512:(b + 1) * 512].bitcast(BF16).rearrange("p (k t) -> p k t", k=8)

        def hres(k, tok0, ntok):
            return [f"h{k}_{t}" for t in range(tok0 // 128, (tok0 + ntok + 127) // 128)]

        final_waits = []
        try:
            const_res = []

            def cload(eng, dst, src, res):
                P.op(eng, lambda e, dst=dst, src=src: e.dma_start(out=dst, in_=src), writes=[res], dma="const_" + eng, serialize=False)
                const_res.append((res, "const_" + eng))

            cload("sp", c_col[:], c_col_d, "c_col")
            cload("sp", b_ada_col[:], b_ada_col_d, "b_ada_col")
            cload("sp", gmix_col[:], gmix_col_d, "gmix_col")
            cload("sp", gffn_col[:], gffn_col_d, "gffn_col")
            cload("sp", b_in_col[:], b_in_col_d, "b_in_col")
            cload("sp", bv_bc[:], bv_bc_d, "bv_bc")
            cload("sp", sink_col[:], sink_col_d, "sink_col")
            cload("sp", convw[:], convw_col_d, "convw")
            cload("sp", halo_flag[:], halo_flag_d, "halo_flag")
            cload("sp", GA1[:], ga1_b_d, "GA1")
            cload("sp", GA2[:], ga2_b_d, "GA2")
            cload("sp", GF[:], gf_bc_d, "GF")
            cload("pool", ident[:], ident_d, "ident")
            cload("pool", mask_cp[:], mask_cp_d, "mask_cp")
            cload("pool", mask_halo[:], mask_halo_d, "mask_halo")
            for res, s in const_res:
                P.lastw[res] = Ev(s, P.dma_cnt[s])

            wada_v = w_ada.rearrange("(k p) n -> p k n", p=128)
            def load_wada(v, after=()):
                for hf in range(2):
                    P.op("pool", lambda e, hf=hf: e.dma_start(out=wada_st[v][:, 4 * hf:4 * hf + 4, :],
                                                              in_=wada_v[:, 4 * hf:4 * hf + 4, v * 1024:(v + 1) * 1024]),
                         reads=list(after), writes=[f"wada_st{v}_{hf}"], dma=f"wst{v}_{hf}")

            NZI = NZ + 2 * NFC
            loaded = set()

            def load_z(zi):
                if zi >= NZI or zi in loaded:
                    return
                loaded.add(zi)
                i = zi % 6
                src_ap = w_in_perm[zi] if zi < NZ else wfi_perm[zi - NZ]
                P.op("pool", lambda e: e.dma_start(out=ws[i], in_=src_ap.rearrange("p (k n) -> p k n", k=8)),
                     writes=[f"ws{i}"], dma=f"ws{i}")

            load_wada(0)
            load_wada(1)
            for zi in range(6):
                load_z(zi)

            P.op("dve", lambda e: e.memset(eps_col[:], EPS), writes=["eps_col"])
            P.op("dve", lambda e: e.memset(ones_bf[:], 1.0), writes=["ones_bf"])
            P.op("act", lambda e: e.activation(out=sc_f[:], in_=c_col[:], func=AF.Silu), reads=["c_col"], writes=["sc_f"])
            P.op("dve", lambda e: e.tensor_copy(out=sc_bf[:], in_=sc_f[:]), reads=["sc_f"], writes=["sc_bf"])
            for k in range(8):
                P.op("dve", lambda e, k=k: e.tensor_scalar(out=sc_rep[:, k, :], in0=ones_bf[:], scalar1=sc_f[:, k:k + 1], scalar2=None, op0=ALU.mult),
                     reads=["sc_f", "ones_bf"], writes=["sc_rep"])
            P.op("dve", lambda e: e.tensor_scalar(out=hb_col[:], in0=b_in_col[:], scalar1=0.5, scalar2=None, op0=ALU.mult),
                 reads=["b_in_col"], writes=["hb_col"])
            P.op("act", lambda e: e.activation(out=esink[:], in_=sink_col[:], func=AF.Exp), reads=["sink_col"], writes=["esink"])

            def mod_columns(v):
                for j in range(8):
                    for k in range(8):
                        P.op("pe", lambda e, j=j, k=k: e.matmul(psum[:, 3 * 512 + v * 8 + j:3 * 512 + v * 8 + j + 1],
                                                                lhsT=wada_st[v][:, k, j * 128:(j + 1) * 128],
                                                                rhs=sc_bf[:, k:k + 1], start=(k == 0), stop=(k == 7)),
                             reads=[f"wada_st{v}_{k // 4}", "sc_bf"], writes=["ps3"], sig=(k == 7 and j == 7))
                P.op("dve", lambda e: e.tensor_tensor(out=modcol[:, v * 8:v * 8 + 8], in0=psum[:, 3 * 512 + v * 8:3 * 512 + v * 8 + 8],
                                                      in1=b_ada_col[:, v * 8:v * 8 + 8], op=ALU.add),
                     reads=["b_ada_col", "ps3"], writes=[f"modcol{v}"])

            def mod_bcast(v, GA, gres, scale):
                for hf in range(2):
                    for k in range(8):
                        P.op("pe", lambda e, hf=hf, k=k: e.matmul(bank(hf), lhsT=sc_rep[:, k, :], rhs=wada_st[v][:, k, hf * 512:(hf + 1) * 512],
                                                                  start=(k == 0), stop=(k == 7)),
                             reads=[f"wada_st{v}_{k // 4}", "sc_rep"], writes=[f"ps{hf}"], sig=(k == 7))
                    P.op("dve", lambda e, hf=hf: e.tensor_tensor(out=GA[:, hf * 512:(hf + 1) * 512], in0=bank(hf),
                                                                 in1=GA[:, hf * 512:(hf + 1) * 512], op=ALU.add),
                         reads=[gres, f"ps{hf}"], writes=[gres])
                if scale != 1.0:
                    P.op("dve", lambda e: e.tensor_scalar(out=GA[:], in0=GA[:], scalar1=scale, scalar2=None, op0=ALU.mult),
                         reads=[gres], writes=[gres])

            mod_columns(0)
            mod_columns(1)
            P.op("dve", lambda e: e.scalar_tensor_tensor(out=a1col[:], in0=modcol[:, 8:16], scalar=1.0, in1=gmix_col[:],
                                                         op0=ALU.add, op1=ALU.mult),
                 reads=["modcol1", "gmix_col"], writes=["a1col"])

            junk_bf = sgt[:].rearrange("p a b -> p (a b)")

            def nt_stage1(src_ap, src_res, ssi):
                P.op("act", lambda e: e.activation(out=junk_bf, in_=src_ap, func=AF.Square, accum_out=ss[:, ssi:ssi + 1]),
                     reads=[src_res], writes=[f"ss{ssi}"])
                P.op("act", lambda e: e.activation(out=sq[:, ssi:ssi + 1], in_=ss[:, ssi:ssi + 1], func=AF.Sqrt,
                                                   bias=eps_col[:], scale=1.0 / D),
                     reads=[f"ss{ssi}", "eps_col"], writes=[f"sq{ssi}"])

            def nt_recip(ssi):
                P.op("dve", lambda e: e.reciprocal(out=rstd[:, ssi:ssi + 1], in_=sq[:, ssi:ssi + 1]),
                     reads=[f"sq{ssi}"], writes=[f"rstd{ssi}"])

            def nt_copy(src_ap, src_res, ssi):
                xi = ssi % 2
                P.op("act", lambda e: e.activation(out=xn[:, xi, :], in_=src_ap, func=AF.Copy, scale=rstd[:, ssi:ssi + 1]),
                     reads=[src_res, f"rstd{ssi}"], writes=[f"xn{xi}"])

            def nt_transposes(ssi, pbank):
                xi = ssi % 2
                pb = bank_bf(pbank)
                for k in range(8):
                    P.op("pe", lambda e, k=k: e.transpose(pb[:, k, :], xn[:, xi, k * 128:(k + 1) * 128], ident[:]),
                         reads=[f"xn{xi}", "ident"], writes=[f"ps{pbank}"], sig=(k == 7))

            def nt_stage2(src_ap, src_res, ssi, pbank):
                nt_recip(ssi)
                nt_copy(src_ap, src_res, ssi)
                nt_transposes(ssi, pbank)

            def nt_stage3(t_h, acol, shcol, acol_res, shcol_res, pbank):
                pb = bank_bf(pbank)
                for k in range(8):
                    P.op("dve", lambda e, k=k: e.tensor_scalar(out=h[:, k, t_h * 128:(t_h + 1) * 128], in0=pb[:, k, :],
                                                               scalar1=acol[:, k:k + 1], scalar2=shcol[:, k:k + 1],
                                                               op0=ALU.mult, op1=ALU.add),
                         reads=[acol_res, shcol_res, f"ps{pbank}"], writes=[f"h{k}_{t_h}"])

            def emit_late_mods():
                mod_columns(3)
                mod_columns(4)
                P.op("dve", lambda e: e.scalar_tensor_tensor(out=a2col[:], in0=modcol[:, 32:40], scalar=1.0, in1=gffn_col[:],
                                                             op0=ALU.add, op1=ALU.mult),
                     reads=["modcol4", "gffn_col"], writes=["a2col"])
                mod_bcast(2, GA1, "GA1", 0.5)
                mod_bcast(5, GA2, "GA2", 1.0)
                tiles = [0, 512, 1024, 1536]
                P.fence([f"cva1_{t}" for t in tiles] + [f"ta1_{t}" for t in tiles] + [f"qT1_{t}" for t in tiles]
                        + [f"mg{c}_{g}" for c in range(8) for g in range(8)] + ["wout0", "wout1"], engines=("pe",))

            checkpoint(1)
            for i in range(17 + 2):
                if i < 17:
                    t = i
                    xi = t % 4
                    P.op("sp", lambda e, t=t, xi=xi: e.dma_start(out=xslot[xi], in_=xh[t * 128:(t + 1) * 128, :]),
                         writes=[f"xb{xi}"], dma=f"xb{xi}")
                    nt_stage1(xslot[xi], f"xb{xi}", t)
                    nt_stage2(xslot[xi], f"xb{xi}", t, 4 + t % 4)
                if 0 <= i - 1 < 17:
                    t = i - 1
                    nt_stage3(t, a1col[:], modcol[:, 0:8], "a1col", "modcol0", 4 + t % 4)
                if i == 9 and LATE_AT_B:
                    emit_late_mods()
            for v in (3, 4, 5, 2):
                load_wada(v, after=[f"xb{i}" for i in range(4)])
            tiles_ = [0, 512, 1024, 1536]
            P.fence([f"cx_{t}" for t in tiles_] + [f"u_{t}" for t in tiles_] + ["u_h"] + [f"cva0_{t}" for t in tiles_]
                    + [f"ta0_{t}" for t in tiles_] + [f"qT0_{t}" for t in tiles_] + [f"rden{i}" for i in range(2)]
                    + [f"nrm{i}" for i in range(2)], engines=("pe",))

            checkpoint(2)
            wout_v = w_out_d.rearrange("(k p) n -> p k n", p=128)

            proj_bank = [0]

            def proj_group(slot, tok0, ntok, evac):
                b = proj_bank[0] % NPROJ_BANKS
                proj_bank[0] += 1
                for k in range(8):
                    P.op("pe", lambda e, k=k: e.matmul(bank(b)[:, 0:ntok], lhsT=ws[slot][:, k, :], rhs=h[:, k, tok0:tok0 + ntok],
                                                       start=(k == 0), stop=(k == 7)),
                         reads=[f"ws{slot}"] + hres(k, tok0, ntok), writes=[f"ps{b}"], sig=(k == 7))
                evac(b)

            for g in range(2):
                slot = g
                for tok0, ntok in [(0, 128), (128, 512), (640, 512), (1152, 512), (1664, 512)]:
                    def ev_k(b, g=g, tok0=tok0, ntok=ntok):
                        P.op("act", lambda e: e.activation(out=kT[:, g, tok0:tok0 + ntok], in_=bank(b)[:, 0:ntok], func=AF.Identity,
                                                           bias=b_in_col[:, g:g + 1]),
                             reads=["b_in_col", f"ps{b}"], writes=[f"kT{g}_{tok0}"])
                    proj_group(slot, tok0, ntok, ev_k)
                load_z(6 + g)

            def kres(g, tb):
                tok = tb * 128
                for tok0, ntok in [(0, 128), (128, 512), (640, 512), (1152, 512), (1664, 512)]:
                    if tok0 <= tok < tok0 + ntok:
                        return f"kT{g}_{tok0}"

            vslot = 2
            for tb in range(17):
                for k in range(8):
                    P.op("pe", lambda e, tb=tb, k=k: e.matmul(bank(3)[:, 0:128], lhsT=h[:, k, tb * 128:(tb + 1) * 128], rhs=ws[vslot][:, k, :],
                                                              start=(k == 0), stop=(k == 7)),
                         reads=[f"ws{vslot}", f"h{k}_{tb}"], writes=["ps3"], sig=(k == 7))
                P.op("dve", lambda e, tb=tb: e.tensor_tensor(out=v_sb[:, tb, :], in0=bank(3)[:, 0:128], in1=bv_bc[:], op=ALU.add),
                     reads=["bv_bc", "ps3"], writes=[f"v{tb}"])

            load_z(8)
            deferred = []
            tap_q = [[], []]
            checkpoint(3)
            TT = [(0, 512), (512, 512), (1024, 512), (1536, 512)]

            def make_chunk_groups(c):
                groups = []
                par = c % 2
                zb = 3 + 6 * c
                def need_slot(name, zi):
                    load_z(zi)
                    return zi % 6

                def prefetch(t):
                    deferred.append(lambda: load_z(zb + 6 + t))

                for (t0, nt) in TT:
                    def g_cx(t0=t0, nt=nt):
                        s = need_slot("cx", zb + 0)

                        def evac(b):
                            P.op("act", lambda e: e.activation(out=cx_t[:, t0:t0 + nt], in_=bank(b), func=AF.Identity,
                                                               bias=b_in_col[:, zb:zb + 1]),
                                 reads=["b_in_col", f"ps{b}"], writes=[f"cx_{t0}"])
                        proj_group(s, 128 + t0, nt, evac)
                    groups.append(g_cx)

                def g_halo():
                    s_cx = need_slot("cx", zb + 0)
                    s_cc = need_slot("cc", zb + 1)
                    for idx, s in enumerate((s_cx, s_cc)):
                        for k in range(8):
                            P.op("pe", lambda e, k=k, s=s, idx=idx: e.matmul(bank(3)[:, 400 + 2 * idx:402 + 2 * idx], lhsT=ws[s][:, k, :],
                                                                              rhs=h[:, k, 126:128], start=(k == 0), stop=(k == 7)),
                                 reads=[f"ws{s}", f"h{k}_0"], writes=["ps3"], sig=(k == 7))
                    P.op("dve", lambda e: e.tensor_scalar(out=small[:, 0:2], in0=bank(3)[:, 400:402], scalar1=b_in_col[:, zb:zb + 1],
                                                          scalar2=halo_flag[:, 0:1], op0=ALU.add, op1=ALU.mult),
                         reads=["b_in_col", "halo_flag", "ps3"], writes=["small01"])
                    P.op("dve", lambda e: e.scalar_tensor_tensor(out=u_t[:, 0:2], in0=bank(3)[:, 402:404], scalar=b_in_col[:, zb + 1:zb + 2],
                                                                 in1=small[:, 0:2], op0=ALU.add, op1=ALU.mult),
                         reads=["b_in_col", "small01", "ps3"], writes=["u_h"])
                    prefetch(0)
                if DO_HALO:
                    groups.append(g_halo)

                for ti, (t0, nt) in enumerate(TT):
                    def g_cc(t0=t0, nt=nt, ti=ti):
                        s = need_slot("cc", zb + 1)

                        def evac(b):
                            P.op("dve", lambda e: e.scalar_tensor_tensor(out=u_t[:, 2 + t0:2 + t0 + nt], in0=bank(b), scalar=b_in_col[:, zb + 1:zb + 2],
                                                                         in1=cx_t[:, t0:t0 + nt], op0=ALU.add, op1=ALU.mult),
                                 reads=["b_in_col", f"cx_{t0}", f"ps{b}"], writes=[f"u_{t0}"])
                            prev = ["u_h"] if ti == 0 else [f"u_{TT[ti - 1][0]}"]
                            cva = cva_t[par]
                            if DEFER_TAPS:
                                tap_q[0].append(lambda: taps(cva, prev))
                            else:
                                taps(cva, prev)

                        def taps(cva, prev):
                            P.op("act", lambda e: e.activation(out=cva[:, t0:t0 + nt], in_=u_t[:, t0:t0 + nt], func=AF.Copy,
                                                               scale=convw[:, c:c + 1]),
                                 reads=[f"u_{t0}", "convw"] + prev, writes=[f"cva{par}_{t0}"])
                            if TAPS_ON_ACT:
                                j = ti % 2
                                for tap in (1, 2):
                                    tmp = xb[:, j, (tap - 1) * 512:(tap - 1) * 512 + nt]
                                    P.op("act", lambda e, tap=tap, tmp=tmp: e.activation(out=tmp, in_=u_t[:, tap + t0:tap + t0 + nt], func=AF.Copy,
                                                                                         scale=convw[:, 8 * tap + c:8 * tap + c + 1]),
                                         reads=[f"u_{t0}", "convw"] + (prev if tap == 1 else []), writes=[f"tp{j}_{tap}"])
                                    P.op("dve", lambda e, tmp=tmp: e.tensor_tensor(out=cva[:, t0:t0 + nt], in0=cva[:, t0:t0 + nt], in1=tmp, op=ALU.add),
                                         reads=[f"tp{j}_{tap}", f"cva{par}_{t0}"], writes=[f"cva{par}_{t0}"])
                            else:
                                P.op("dve", lambda e: e.scalar_tensor_tensor(out=cva[:, t0:t0 + nt], in0=u_t[:, 1 + t0:1 + t0 + nt],
                                                                             scalar=convw[:, 8 + c:9 + c], in1=cva[:, t0:t0 + nt],
                                                                             op0=ALU.mult, op1=ALU.add),
                                     reads=[f"u_{t0}", "convw", f"cva{par}_{t0}"] + prev, writes=[f"cva{par}_{t0}"])
                                P.op("dve", lambda e: e.scalar_tensor_tensor(out=cva[:, t0:t0 + nt], in0=u_t[:, 2 + t0:2 + t0 + nt],
                                                                             scalar=convw[:, 16 + c:17 + c], in1=cva[:, t0:t0 + nt],
                                                                             op0=ALU.mult, op1=ALU.add),
                                     reads=[f"u_{t0}", "convw", f"cva{par}_{t0}"], writes=[f"cva{par}_{t0}"])
                        proj_group(s, 128 + t0, nt, evac)
                        if ti == 3:
                            prefetch(1)
                    groups.append(g_cc)

                for (t0, nt) in TT:
                    def g_cb(t0=t0, nt=nt):
                        s = need_slot("cb", zb + 2)

                        def evac(b):
                            cva = cva_t[par]
                            P.op("dve", lambda e: e.scalar_tensor_tensor(out=cva[:, t0:t0 + nt], in0=bank(b), scalar=b_in_col[:, zb + 2:zb + 3],
                                                                         in1=cva[:, t0:t0 + nt], op0=ALU.add, op1=ALU.mult),
                                 reads=["b_in_col", f"cva{par}_{t0}", f"ps{b}"], writes=[f"cva{par}_{t0}"])
                        proj_group(s, 128 + t0, nt, evac)
                        if t0 == 1536:
                            prefetch(2)
                    groups.append(g_cb)

                for (t0, nt) in TT:
                    def g_gc(t0=t0, nt=nt):
                        s = need_slot("gc", zb + 3)

                        def evac(b):
                            cva = cva_t[par]
                            P.op("act", lambda e: e.activation(out=cx_t[:, t0:t0 + nt], in_=bank(b), func=AF.Tanh, scale=0.5,
                                                               bias=hb_col[:, zb + 3:zb + 4]),
                                 reads=["hb_col", f"ps{b}"], writes=[f"cx_{t0}"])
                            P.op("dve", lambda e: e.scalar_tensor_tensor(out=cva[:, t0:t0 + nt], in0=cx_t[:, t0:t0 + nt], scalar=1.0,
                                                                         in1=cva[:, t0:t0 + nt], op0=ALU.add, op1=ALU.mult),
                                 reads=[f"cx_{t0}", f"cva{par}_{t0}"], writes=[f"cva{par}_{t0}"])
                        proj_group(s, 128 + t0, nt, evac)
                        if t0 == 1536:
                            prefetch(3)
                    groups.append(g_gc)

                for (t0, nt) in TT:
                    def g_ga(t0=t0, nt=nt):
                        s = need_slot("ga", zb + 4)

                        def evac(b):
                            P.op("act", lambda e: e.activation(out=ta_t[par][:, t0:t0 + nt], in_=bank(b), func=AF.Tanh, scale=0.5,
                                                               bias=hb_col[:, zb + 4:zb + 5]),
                                 reads=["hb_col", f"ps{b}"], writes=[f"ta{par}_{t0}"])
                        proj_group(s, 128 + t0, nt, evac)
                        if t0 == 1536:
                            prefetch(4)
                    groups.append(g_ga)

                for (t0, nt) in TT:
                    def g_q(t0=t0, nt=nt):
                        s = need_slot("q", zb + 5)

                        def evac(b):
                            P.op("act", lambda e: e.activation(out=qT[par][:, t0:t0 + nt], in_=bank(b), func=AF.Identity,
                                                               bias=b_in_col[:, zb + 5:zb + 6]),
                                 reads=["b_in_col", f"ps{b}"], writes=[f"qT{par}_{t0}"])
                        proj_group(s, 128 + t0, nt, evac)
                        if t0 == 1536:
                            prefetch(5)
                    groups.append(g_q)
                return groups

            pT_ctr = [0]

            def make_attn_steps(c):
                par = c % 2
                g = c // 4
                state = {}

                def scores(kb):
                    slot = pT_ctr[0] % NPT
                    pT_ctr[0] += 1
                    state[kb] = slot
                    parts = ([0] if kb >= 0 else []) + ([1] if kb + 1 <= 15 else [])
                    c0, c1 = parts[0] * 128, parts[-1] * 128 + 128
                    qb0 = kb if kb >= 0 else 0
                    qtok0 = qb0 * 128
                    nq = c1 - c0
                    qres = [f"qT{par}_{((qtok0 + i * 128) // 512) * 512}" for i in range(nq // 128)]
                    sb0 = 2 if (c == 7 and TAIL_DB and kb % 2 == 0) else 4
                    for hd in range(2):
                        r0 = hd * 64
                        P.op("pe", lambda e, hd=hd, r0=r0: e.matmul(bank(sb0 + hd)[:, c0:c1], lhsT=kT[r0:r0 + 64, g, (kb + 1) * 128:(kb + 2) * 128],
                                                                    rhs=qT[par][r0:r0 + 64, qtok0:qtok0 + nq], start=True, stop=True,
                                                                    tile_position=(r0, 0)),
                             reads=[kres(g, kb + 1)] + qres, writes=[f"ps{sb0 + hd}"], sig=True)
                    if EXP_MERGE:
                        src2 = bank(sb0, 2).rearrange("p (h q) -> p h q", h=2)[:, :, c0:c1]
                        P.op("act", lambda e: e.activation(out=pT[slot][:, :, c0:c1], in_=src2, func=AF.Exp, scale=0.125),
                             reads=[f"ps{sb0}", f"ps{sb0 + 1}"], writes=[f"pT{slot}_0", f"pT{slot}_1"])
                    else:
                        for hd in range(2):
                            P.op("act", lambda e, hd=hd: e.activation(out=pT[slot][:, hd, c0:c1], in_=bank(sb0 + hd)[:, c0:c1], func=AF.Exp, scale=0.125),
                                 reads=[f"ps{sb0 + hd}"], writes=[f"pT{slot}_{hd}"])
                    msk = mask3[:, :, c0:c1] if kb >= 0 else mask_halo3
                    P.op(MASK_ENG, lambda e: e.tensor_tensor(out=pT[slot][:, :, c0:c1], in0=pT[slot][:, :, c0:c1],
                                                             in1=msk.to_broadcast([128, 2, c1 - c0]), op=ALU.mult),
                         reads=["mask_cp", "mask_halo"], writes=[f"pT{slot}_0", f"pT{slot}_1"])

                def pv(kb):
                    slot = state[kb]
                    if PV_MERGE and kb >= 0 and kb % 2 == 0:
                        g2 = kb // 2
                        b = 6 + g2 % 2
                        for (lhs_fn, col0, res) in ((lambda: v_sb[:, kb + 1, g * 64:(g + 1) * 64], 0, f"v{kb + 1}"),
                                                    (lambda: ones_bf[:, 0:64], 256, "ones_bf")):
                            for hd in range(2):
                                r0 = hd * 64
                                P.op("pe", lambda e, hd=hd, r0=r0, lhs_fn=lhs_fn, col0=col0: e.matmul(
                                         bank(b)[r0:r0 + 64, col0:col0 + 256], lhsT=lhs_fn(), rhs=pT[slot][:, hd, 0:256],
                                         start=False, stop=False, skip_group_check=True, tile_position=(0, r0)),
                                     reads=[res, f"pT{slot}_{hd}"], writes=[f"ps{b}"], sig=(col0 == 256 and hd == 1))
                        return
                    parts = ([(0, kb)] if kb >= 0 else []) + ([(1, kb + 1)] if kb + 1 <= 15 else [])
                    for (pi, n) in parts:
                        pv_part(kb, slot, pi, n)

                def pv_part(kb, slot, pi, n):
                    if True:
                        g2 = n // 2
                        b = 6 + g2 % 2
                        ocol = (n % 2) * 128
                        dcol = 256 + ocol
                        first_in_bank = (pi == 1 and n % 2 == 0)
                        for hd in range(2):
                            r0 = hd * 64
                            P.op("pe", lambda e, hd=hd, r0=r0: e.matmul(bank(b)[r0:r0 + 64, ocol:ocol + 128], lhsT=v_sb[:, kb + 1, g * 64:(g + 1) * 64],
                                                                        rhs=pT[slot][:, hd, pi * 128:(pi + 1) * 128], start=first_in_bank, stop=(pi == 0),
                                                                        skip_group_check=True, tile_position=(0, r0)),
                                 reads=[f"v{kb + 1}", f"pT{slot}_{hd}"], writes=[f"ps{b}"], sig=False)
                        for hd in range(2):
                            r0 = hd * 64
                            P.op("pe", lambda e, hd=hd, r0=r0: e.matmul(bank(b)[r0:r0 + 64, dcol:dcol + 128], lhsT=ones_bf[:, 0:64],
                                                                        rhs=pT[slot][:, hd, pi * 128:(pi + 1) * 128], start=False, stop=(pi == 0),
                                                                        skip_group_check=True, tile_position=(0, r0)),
                                 reads=["ones_bf", f"pT{slot}_{hd}"], writes=[f"ps{b}"], sig=(hd == 1))
                        if pi == 0 and n % 2 == 1 and DO_NORM:
                            tok0 = g2 * 256
                            i2 = g2 % 2
                            tt0 = (tok0 // 512) * 512
                            P.op("act", lambda e: e.activation(out=rden_t[i2], in_=bank(b)[:, 256:512], func=AF.Identity,
                                                               bias=esink[:, c:c + 1]),
                                 reads=["esink", f"ps{b}"], writes=[f"rden{i2}"])
                            if ACT_OEVAC:
                                P.op("act", lambda e: e.activation(out=nrm_t[i2], in_=bank(b)[:, 0:256], func=AF.Copy),
                                     reads=[f"ps{b}"], writes=[f"nrm{i2}"])
                            P.op("dve", lambda e: e.reciprocal(out=rden_t[i2], in_=rden_t[i2]), reads=[f"rden{i2}"], writes=[f"rden{i2}"])
                            if ACT_OEVAC:
                                P.op("pool" if c == 7 else "dve",
                                     lambda e: e.tensor_tensor(out=nrm_t[i2], in0=nrm_t[i2], in1=rden_t[i2], op=ALU.mult),
                                     reads=[f"rden{i2}", f"nrm{i2}"], writes=[f"nrm{i2}"])
                            else:
                                P.op("dve", lambda e: e.tensor_tensor(out=nrm_t[i2], in0=bank(b)[:, 0:256], in1=rden_t[i2], op=ALU.mult),
                                     reads=[f"rden{i2}", f"ps{b}"], writes=[f"nrm{i2}"])
                            P.op("dve", lambda e: e.scalar_tensor_tensor(out=nrm_t[i2], in0=ta_t[par][:, tok0:tok0 + 256], scalar=1.0,
                                                                         in1=nrm_t[i2], op0=ALU.add, op1=ALU.mult),
                                 reads=[f"ta{par}_{tt0}", f"nrm{i2}"], writes=[f"nrm{i2}"])
                            P.op("pool" if c == 7 else "dve",
                                 lambda e: e.tensor_tensor(out=mergedT[:, c, tok0:tok0 + 256], in0=nrm_t[i2],
                                                           in1=cva_t[par][:, tok0:tok0 + 256], op=ALU.add),
                                 reads=[f"nrm{i2}", f"cva{par}_{tt0}"], writes=[f"mg{c}_{g2}"])

                steps = []
                for i in range(-1, 16 + PV_LAG):
                    def st(i=i):
                        if i <= 15:
                            scores(i)
                        if -1 <= i - PV_LAG <= 15 and DO_PV:
                            pv(i - PV_LAG)
                    steps.append(st)
                return steps

            for c in range(9):
                if c > NCHUNK_DBG:
                    break
                groups = make_chunk_groups(c) if c < min(8, NCHUNK_DBG) else []
                steps = make_attn_steps(c - 1) if (c >= 1 and DO_ATTN) else []
                n = max(len(groups), len(steps))
                for i in range(n):
                    old = list(deferred) + tap_q[1]
                    del deferred[:]
                    tap_q[1] = tap_q[0]
                    tap_q[0] = []
                    if i < len(groups):
                        groups[i]()
                    for f in old:
                        f()
                    if i < len(steps):
                        steps[i]()
                    if c == 2 and i % 3 == 0 and i // 3 < 8:
                        kk = i // 3
                        P.op("pool", lambda e, kk=kk: e.tensor_tensor(out=wout[:, kk, :], in0=wout[:, kk, :], in1=GA1[:], op=ALU.mult),
                             reads=["GA1"], writes=[f"wout{kk // 4}"])
                for f in deferred + tap_q[1] + tap_q[0]:
                    f()
                del deferred[:]
                tap_q[0], tap_q[1] = [], []
                if c == 7 and STAGE >= 4:
                    P.fence([f"x1_{n}" for n in range(9)])
                    for n in range(9):
                        P.op("sp", lambda e, n=n: e.dma_start(out=x1[:, n, :], in_=xh[128 + n * 128:128 + (n + 1) * 128, :]),
                             writes=[f"x1_{n}"], dma=f"xr{n % 4}")
                if c == 0:
                    if not LATE_AT_B:
                        emit_late_mods()
                    for hf in range(2):
                        P.op("pool", lambda e, hf=hf: e.dma_start(out=wout[:, 4 * hf:4 * hf + 4, :], in_=wout_v[:, 4 * hf:4 * hf + 4, :]),
                             writes=[f"wout{hf}"], dma=f"wout{hf}")

            sh2col = modcol[:, 24:32]
            sh2res = "modcol3"

            checkpoint(4)
            P.fence([f"x1_{n}" for n in range(9, NB)])
            for n in range(9, NB):
                P.op("sp", lambda e, n=n: e.dma_start(out=x1[:, n, :], in_=xh[128 + n * 128:128 + (n + 1) * 128, :]),
                     writes=[f"x1_{n}"], dma=f"xr{n % 4}")
            def d_mm(n):
                b0 = 2 * (n % 2)
                for k in range(8):
                    for hf in range(2):
                        P.op("pe", lambda e, k=k, hf=hf: e.matmul(bank(b0 + hf), lhsT=mergedT[:, k, n * 128:(n + 1) * 128],
                                                                   rhs=wout[:, k, hf * 512:(hf + 1) * 512], start=(k == 0), stop=(k == 7)),
                             reads=[f"mg{k}_{n // 2}", f"wout{k // 4}"], writes=[f"ps{b0 + hf}"], sig=(k == 7 and hf == 1))

            def d_resid(n):
                b0 = 2 * (n % 2)
                P.op("dve", lambda e: e.tensor_tensor(out=x1[:, n, :], in0=bank(b0, 2), in1=x1[:, n, :], op=ALU.add),
                     reads=[f"ps{b0}", f"ps{b0 + 1}"], writes=[f"x1_{n}"])

            for i in range(NB + 6):
                def ok(j):
                    return 0 <= j < NB
                if ok(i):
                    d_mm(i)
                if ok(i - 1):
                    d_resid(i - 1)
                if ok(i - 2):
                    n = i - 2
                    nt_stage1(x1[:, n, :], f"x1_{n}", 20 + n)
                if ok(i - 3):
                    nt_recip(20 + i - 3)
                if ok(i - 4):
                    n = i - 4
                    nt_copy(x1[:, n, :], f"x1_{n}", 20 + n)
                if ok(i - 5):
                    n = i - 5
                    nt_transposes(20 + n, 4 + n % 4)
                if ok(i - 6):
                    n = i - 6
                    nt_stage3(n, a2col[:], sh2col, "a2col", sh2res, 4 + n % 4)

            checkpoint(5)
            P.fence([f"aT{jl}_{tt}" for jl in range(8) for tt in range(4)] + [f"wfo{i}" for i in range(8)])
            pair_ctr = [0]
            for G, js in enumerate(FGROUPS):
                for jl, j in enumerate(js):
                    load_z(NZ + 2 * j)
                    load_z(NZ + 2 * j + 1)
                    sg_ = (NZ + 2 * j) % 6
                    su_ = (NZ + 2 * j + 1) % 6
                    if jl == 2:
                        for jl2, j2 in enumerate(js):
                            P.op("pool", lambda e, jl2=jl2, j2=j2: e.dma_start(out=wfo[jl2], in_=w_ffn_out[j2 * 128:(j2 + 1) * 128, :]),
                                 writes=[f"wfo{jl2}"], dma=f"wfo{jl2}")
                        for jl2, j2 in enumerate(js):
                            P.op("pool", lambda e, jl2=jl2: e.tensor_tensor(out=wfo[jl2], in0=wfo[jl2], in1=GA2[:], op=ALU.mult),
                                 reads=["GA2"], writes=[f"wfo{jl2}"])
                    for tt in range(4):
                        bp = pair_ctr[0] % 4
                        pair_ctr[0] += 1
                        bg, bu = 2 * bp, 2 * bp + 1
                        for (s_, b_) in ((sg_, bg), (su_, bu)):
                            for k in range(8):
                                P.op("pe", lambda e, k=k, s_=s_, b_=b_, tt=tt: e.matmul(bank(b_), lhsT=ws[s_][:, k, :], rhs=h[:, k, tt * 512:(tt + 1) * 512],
                                                                                         start=(k == 0), stop=(k == 7)),
                                     reads=[f"ws{s_}"] + hres(k, tt * 512, 512), writes=[f"ps{b_}"], sig=(k == 7))
                        si_ = pair_ctr[0] % 2
                        P.op("act", lambda e, si_=si_, bg=bg: e.activation(out=sgt[:, si_, :], in_=bank(bg), func=AF.Silu),
                             reads=[f"ps{bg}"], writes=[f"sgt{si_}"])
                        P.op("dve", lambda e, si_=si_, bu=bu, jl=jl, tt=tt: e.tensor_tensor(out=aT[:, jl, tt * 512:(tt + 1) * 512], in0=bank(bu),
                                                                                             in1=sgt[:, si_, :], op=ALU.mult),
                             reads=[f"sgt{si_}", f"ps{bu}"], writes=[f"aT{jl}_{tt}"])
                last = (G == len(FGROUPS) - 1)
                nj = len(js)

                def f_mm(n, nj=nj):
                    b0 = 2 * ((pair_base + n) % 4)
                    for jl in range(nj):
                        for hf in range(2):
                            P.op("pe", lambda e, jl=jl, hf=hf: e.matmul(bank(b0 + hf), lhsT=aT[:, jl, n * 128:(n + 1) * 128],
                                                                         rhs=wfo[jl][:, hf * 512:(hf + 1) * 512],
                                                                         start=(jl == 0), stop=(jl == nj - 1)),
                                 reads=[f"aT{jl}_{n // 4}", f"wfo{jl}"], writes=[f"ps{b0 + hf}"], sig=(jl == nj - 1 and hf == 1))

                def f_resid(n):
                    b0 = 2 * ((pair_base + n) % 4)
                    P.op("dve", lambda e: e.tensor_tensor(out=x1[:, n, :], in0=bank(b0, 2), in1=x1[:, n, :], op=ALU.add),
                         reads=[f"ps{b0}", f"ps{b0 + 1}"], writes=[f"x1_{n}"])

                def f_stats(n):
                    si = 40 + n
                    junk2 = xn[:].rearrange("p a b -> p (a b)")[:, 0:1024]
                    P.op("act", lambda e: e.activation(out=junk2, in_=x1[:, n, :], func=AF.Square, accum_out=ss[:, si:si + 1]),
                         reads=[f"x1_{n}"], writes=[f"ss{si}"])
                    P.op("act", lambda e: e.activation(out=sq[:, si:si + 1], in_=ss[:, si:si + 1], func=AF.Sqrt,
                                                       bias=eps_col[:], scale=1.0 / D),
                         reads=[f"ss{si}", "eps_col"], writes=[f"sq{si}"])

                def f_out(n):
                    si = 40 + n
                    P.op("dve", lambda e: e.reciprocal(out=rstd[:, si:si + 1], in_=sq[:, si:si + 1]),
                         reads=[f"sq{si}"], writes=[f"rstd{si}"])
                    P.op("dve", lambda e: e.scalar_tensor_tensor(out=x1[:, n, :], in0=x1[:, n, :], scalar=rstd[:, si:si + 1],
                                                                 in1=GF[:], op0=ALU.mult, op1=ALU.mult),
                         reads=[f"rstd{si}", "GF"], writes=[f"x1_{n}"])
                    P.op("sp", lambda e: e.dma_start(out=out_d[n * 128:(n + 1) * 128, :], in_=x1[:, n, :]),
                         reads=[f"x1_{n}"], writes=[f"out{n}"], dma=f"od{n % 4}")

                pair_base = pair_ctr[0]
                pair_ctr[0] += NB
                for i in range(NB + 3):
                    if 0 <= i < NB:
                        f_mm(i)
                    if 0 <= i - 1 < NB:
                        f_resid(i - 1)
                    if last and 0 <= i - 2 < NB:
                        f_stats(i - 2)
                    if last and 0 <= i - 3 < NB:
                        f_out(i - 3)

            final_waits = [(f"od{i}", P.dma_cnt[f"od{i}"]) for i in range(4)]
        except StopEmit:
            pass


        sem_names = list(Prog.ENGS) + sorted(P.dma_cnt.keys())
        sems = {}
        for s in sem_names:
            sems[s] = es.enter_context(nc.semaphore("s_" + s))
        block = es.enter_context(nc.Block())

        @block.tensor
        def _(eng):
            P.replay("pe", eng, sems)

        @block.scalar
        def _(eng):
            P.replay("act", eng, sems)

        @block.vector
        def _(eng):
            P.replay("dve", eng, sems)

        @block.gpsimd
        def _(eng):
            P.replay("pool", eng, sems)

        @block.sync
        def _(eng):
            P.replay("sp", eng, sems)
            for s, v in final_waits:
                eng.wait_ge(sems[s], v)
    stats = {e: len(P.lists[e]) for e in Prog.ENGS}
    stats["nsems"] = len(sem_names)
    return nc, stats


def _col(vec, nchunks):
    return np.ascontiguousarray(np.asarray(vec, np.float32).reshape(nchunks, 128).T)


def prepare_inputs(x, c, w_ada, b_ada, g_mix, w_in, b_in, sinks, conv_w, w_out, g_ffn, w_ffn_in, w_ffn_out, g_final):
    f32 = np.float32
    x = np.asarray(x, f32); c = np.asarray(c, f32)
    w_ada = np.ascontiguousarray(np.asarray(w_ada, f32)[0]); b_ada = np.asarray(b_ada, f32)[0]
    g_mix = np.asarray(g_mix, f32)[0]; w_in = np.asarray(w_in, f32)[0]; b_in = np.asarray(b_in, f32)[0]
    sinks = np.asarray(sinks, f32)[0]; conv_w = np.asarray(conv_w, f32)[0]
    w_out = np.ascontiguousarray(np.asarray(w_out, f32)[0]); g_ffn = np.asarray(g_ffn, f32)[0]
    w_ffn_in = np.asarray(w_ffn_in, f32)[0]; w_ffn_out = np.ascontiguousarray(np.asarray(w_ffn_out, f32)[0])
    g_final = np.asarray(g_final, f32)

    Q0, K0, V0 = 0, 1024, 1152
    CB0, CC0, CX0, GA0, GC0 = 1280, 2304, 3328, 4352, 5376
    cols = []
    for g in range(2):
        kc = np.arange(K0 + g * 64, K0 + (g + 1) * 64)
        cols.append(np.concatenate([kc, kc]))
    cols.append(np.arange(V0, V0 + 128))
    for ch in range(8):
        r = np.arange(ch * 128, (ch + 1) * 128)
        for base in (CX0, CC0, CB0, GC0, GA0, Q0):
            cols.append(base + r)
    cols = np.stack(cols)
    w_in_k = w_in.reshape(8, 128, -1)
    w_in_perm = np.ascontiguousarray(np.transpose(w_in_k[:, :, cols], (2, 1, 0, 3)).reshape(NZ, 128, 1024))
    b_in_col = np.ascontiguousarray(b_in[cols].T)
    bv_bc = np.ascontiguousarray(np.broadcast_to(b_in[V0:V0 + 128][None, :], (128, 128)))
    fcols = []
    for j in range(NFC):
        fcols.append(np.arange(j * 128, (j + 1) * 128))
        fcols.append(DFF + np.arange(j * 128, (j + 1) * 128))
    fcols = np.stack(fcols)
    wfi_k = w_ffn_in.reshape(8, 128, -1)
    wfi_perm = np.ascontiguousarray(np.transpose(wfi_k[:, :, fcols], (2, 1, 0, 3)).reshape(2 * NFC, 128, 1024))

    heads = (2 * np.arange(8)[None, :] + (np.arange(128)[:, None] >= 64)).astype(np.int64)
    sink_col = np.ascontiguousarray(sinks[heads])
    convw_col = np.ascontiguousarray(np.concatenate([_col(conv_w[k], 8) for k in range(3)], axis=1))
    kk = np.arange(128)[:, None]; qq = np.arange(128)[None, :]
    tri_cur = (kk <= qq).astype(f32)
    tri_prev = (kk > qq).astype(f32)
    mask_cp = np.ascontiguousarray(np.concatenate([tri_cur, tri_prev], axis=1))
    b_ada_col = np.ascontiguousarray(b_ada.reshape(48, 128).T)
    shared = dict(
        w_ada=w_ada, b_ada_col=b_ada_col, gmix_col=_col(g_mix, 8), gffn_col=_col(g_ffn, 8),
        w_in_perm=w_in_perm, b_in_col=b_in_col, bv_bc=bv_bc, sink_col=sink_col, convw_col=convw_col,
        mask_cp=mask_cp, ident=np.eye(128, dtype=f32), w_out=w_out,
        ga1_b_bc=np.ascontiguousarray(np.broadcast_to(b_ada[2048:3072][None, :], (128, D))),
        ga2_b_bc=np.ascontiguousarray(np.broadcast_to(b_ada[5120:6144][None, :], (128, D))),
        gf_bc=np.ascontiguousarray(np.broadcast_to(g_final[None, :], (128, D))),
        wfi_perm=wfi_perm, w_ffn_out=w_ffn_out,
    )
    in_maps = []
    for i in range(NCORES):
        b, qtr = i // 4, i % 4
        s = qtr * NT
        xh = np.zeros((NTH, D), f32)
        xh[128:] = x[b, s:s + NT]
        first = (qtr == 0)
        if not first:
            xh[:128] = x[b, s - 128:s]
        m = dict(shared)
        m["xh"] = xh
        m["c_col"] = _col(c[b], 8)
        m["halo_flag"] = np.full((128, 1), 0.0 if first else 1.0, f32)
        m["mask_halo"] = np.zeros((128, 128), f32) if first else tri_prev.copy()
        in_maps.append(m)
    return in_maps


_CACHE = {}


def kernel(x, c, w_ada, b_ada, g_mix, w_in, b_in, sinks, conv_w, w_out, g_ffn, w_ffn_in, w_ffn_out, g_final):
    in_maps = prepare_inputs(x, c, w_ada, b_ada, g_mix, w_in, b_in, sinks, conv_w, w_out, g_ffn, w_ffn_in, w_ffn_out, g_final)
    if "nc" not in _CACHE:
        _CACHE["nc"] = build_program()[0]
    nc = _CACHE["nc"]
    res = run_bass_kernel_spmd(nc, in_maps, core_ids=list(range(NCORES)))
    out = np.empty((2, 4 * NT, D), np.float32)
    for i in range(NCORES):
        out[i // 4, (i % 4) * NT:(i % 4 + 1) * NT] = res.results[i]["out"]
    return out
```
